# Optimizing a Trainium2 kernel written in Bass

```python
import jax, jax.numpy as jnp
from jax import lax
import numpy as np

D_MODEL = 1024
BATCH = 2
SEQ = 8192
DEPTH = 2
DEC_BATCH = 4
DEC_SEQ = 8192
PAST_LEN = 128

N_EVEN = (DEPTH + 1) // 2
N_ODD = DEPTH // 2
D_FF = 4 * D_MODEL
RMS_EPS = 1e-6
GRID_W = 64
Q_BLOCK = 128
RWKV_DIM = D_MODEL // 2
RWKV_HEAD = 64
RWKV_HEADS = RWKV_DIM // RWKV_HEAD
DECAY_LORA = 64
ICLR_LORA = 64
GATE_LORA = 128
GN_EPS = 64e-5
HEAD_DIM = 64
N_Q_HEADS = (D_MODEL // 2) // HEAD_DIM
N_KV_HEADS = 2
GQA_GROUP = N_Q_HEADS // N_KV_HEADS
ROPE_THETA = 10000.0
AXIS_PAIRS = HEAD_DIM // 4
S5_GROUP = 16
S5_GROUPS = D_MODEL // S5_GROUP
S5_STATE = 64
RWKV_SPLITS = (RWKV_DIM, RWKV_DIM, RWKV_DIM, DECAY_LORA, DECAY_LORA, ICLR_LORA, GATE_LORA)
RWKV_IN_WIDTH = sum(RWKV_SPLITS)
ATT_SPLITS = (N_Q_HEADS * HEAD_DIM, N_KV_HEADS * HEAD_DIM, N_KV_HEADS * HEAD_DIM)
IN_WIDTH = RWKV_IN_WIDTH + sum(ATT_SPLITS)

kernel_name = 'hybrid_rwkv7_axialgqa_s5_encoder'


def _split(z, sizes):
    return jnp.split(z, [int(s) for s in np.cumsum(sizes)[:-1]], axis=-1)


def _rms_norm(x, g):
    xf = x.astype(jnp.float32)
    y = xf * lax.rsqrt(jnp.mean(jnp.square(xf), axis=-1, keepdims=True) + RMS_EPS)
    return (y * g.astype(jnp.float32)).astype(x.dtype)


def _sqrelu_mlp(h, w_up, w_down):
    return jnp.square(jax.nn.relu(h @ w_up)) @ w_down


def _centred_shift(h, mu):
    prev = jnp.pad(h[:, :-1], ((0, 0), (1, 0), (0, 0)))
    nxt = jnp.pad(h[:, 1:], ((0, 0), (0, 1), (0, 0)))
    return h + mu * (0.5 * (prev + nxt) - h)


def _wkv_scan(r, w, k, v, a, b):
    bsz, _, nh, n = r.shape
    xs = tuple(jnp.moveaxis(z.astype(jnp.float32), 1, 0) for z in (r, w, k, v, a, b))

    def step(S, inp):
        r_t, w_t, k_t, v_t, a_t, b_t = inp
        sa = jnp.einsum('bhvk,bhk->bhv', S, a_t)
        S = S * w_t[:, :, None, :] + sa[..., None] * b_t[:, :, None, :] + v_t[..., None] * k_t[:, :, None, :]
        return S, jnp.einsum('bhvk,bhk->bhv', S, r_t)

    S0 = jnp.zeros((bsz, nh, n, n), jnp.float32)
    _, y = lax.scan(step, S0, xs)
    return jnp.moveaxis(y, 0, 1)


def _rwkv7_bidir(h, p, i):
    bsz, T, _ = h.shape
    f32 = jnp.float32
    h = _centred_shift(h, p['hyb_shift_mu'][i])
    r, k, v, hw_f, hw_b, ha, hg = _split(h, RWKV_SPLITS)

    def heads(z):
        return z.astype(f32).reshape(bsz, T, RWKV_HEADS, RWKV_HEAD)

    def decay(w0, w_up, hw):
        wl = (w0 + jnp.tanh(hw) @ w_up).astype(f32)
        return heads(jnp.exp(-jnp.exp(-jax.nn.softplus(-wl) - 0.5)))

    w_f = decay(p['rwkv_w0_f'][i], p['rwkv_w_up_f'][i], hw_f)
    w_b = decay(p['rwkv_w0_b'][i], p['rwkv_w_up_b'][i], hw_b)
    a = jax.nn.sigmoid((p['rwkv_a0'][i] + ha @ p['rwkv_a_up'][i]).astype(f32))
    g = jax.nn.sigmoid(hg) @ p['rwkv_g_up'][i]
    kk = heads(k * p['rwkv_k_k'][i])
    kk = kk / jnp.maximum(jnp.sqrt(jnp.sum(jnp.square(kk), axis=-1, keepdims=True)), 1e-12)
    k = heads(k.astype(f32) * (1.0 + (a - 1.0) * p['rwkv_k_a'][i].astype(f32)))
    r = heads(r)
    v = heads(v)
    rem = -kk
    add = kk * heads(a)
    flip = lambda z: jnp.flip(z, axis=1)
    y = _wkv_scan(r, w_f, k, v, rem, add) + flip(
        _wkv_scan(flip(r), flip(w_b), flip(k), flip(v), flip(rem), flip(add)))
    mean = jnp.mean(y, axis=-1, keepdims=True)
    var = jnp.mean(jnp.square(y - mean), axis=-1, keepdims=True)
    y = ((y - mean) * lax.rsqrt(var + GN_EPS)).reshape(bsz, T, RWKV_DIM)
    y = y * p['rwkv_lnx_g'][i].astype(f32) + p['rwkv_lnx_b'][i].astype(f32)
    bonus = jnp.sum(r * k * p['rwkv_r_k'][i].astype(f32), axis=-1, keepdims=True) * v
    y = (y + bonus.reshape(bsz, T, RWKV_DIM)) * g.astype(f32)
    return y.astype(h.dtype)


def _axial_rope(T):
    rows = T // GRID_W
    row_ids = jnp.repeat(jnp.arange(rows, dtype=jnp.float32), GRID_W)
    col_ids = jnp.tile(jnp.arange(GRID_W, dtype=jnp.float32), rows)
    inv_freq = ROPE_THETA ** (-jnp.arange(AXIS_PAIRS, dtype=jnp.float32) / AXIS_PAIRS)
    ang = jnp.concatenate([row_ids[:, None] * inv_freq, col_ids[:, None] * inv_freq], axis=-1)
    return jnp.cos(ang), jnp.sin(ang)


def _apply_rope(x, cos, sin):
    xf = x.astype(jnp.float32).reshape(*x.shape[:-1], HEAD_DIM // 2, 2)
    x1, x2 = xf[..., 0], xf[..., 1]
    c = cos[None, :, None, :]
    s = sin[None, :, None, :]
    out = jnp.stack([x1 * c - x2 * s, x1 * s + x2 * c], axis=-1)
    return out.reshape(x.shape).astype(x.dtype)


def _block_attention(q, k, v):
    bsz, T = q.shape[:2]
    nb = T // Q_BLOCK
    qb = q.reshape(bsz, nb, Q_BLOCK, N_KV_HEADS, GQA_GROUP, HEAD_DIM).transpose(1, 0, 2, 3, 4, 5)
    scale = HEAD_DIM ** -0.5

    def one_block(qblk):
        s = jnp.einsum('bqhgd,bkhd->bhgqk', qblk, k).astype(jnp.float32) * scale
        pr = jax.nn.softmax(s, axis=-1).astype(v.dtype)
        return jnp.einsum('bhgqk,bkhd->bqhgd', pr, v)

    o = lax.map(one_block, qb)
    return o.transpose(1, 0, 2, 3, 4, 5).reshape(bsz, T, N_Q_HEADS * HEAD_DIM)


def _axial_gqa(h, p, i):
    bsz, T, _ = h.shape
    q, k, v = _split(h, ATT_SPLITS)
    q = _rms_norm(q.reshape(bsz, T, N_Q_HEADS, HEAD_DIM), p['att_q_norm'][i])
    k = _rms_norm(k.reshape(bsz, T, N_KV_HEADS, HEAD_DIM), p['att_k_norm'][i])
    v = v.reshape(bsz, T, N_KV_HEADS, HEAD_DIM)
    cos, sin = _axial_rope(T)
    return _block_attention(_apply_rope(q, cos, sin), _apply_rope(k, cos, sin), v)


def _hybrid_mixer(hn, p, i):
    h = hn @ p['hyb_w_in'][i]
    y_a = _rwkv7_bidir(h[..., :RWKV_IN_WIDTH], p, i)
    y_b = _axial_gqa(h[..., RWKV_IN_WIDTH:], p, i)
    return jnp.concatenate([y_a, y_b], axis=-1) @ p['hyb_w_out'][i]


def _s5_scan(bu_re, bu_im, lam_re, lam_im, log_dt, reverse):
    dt = jnp.exp(log_dt.astype(jnp.float32))[:, None]
    lam_re = lam_re.astype(jnp.float32)
    lam_im = lam_im.astype(jnp.float32)
    mag = jnp.exp(lam_re * dt)
    lb_re, lb_im = mag * jnp.cos(lam_im * dt), mag * jnp.sin(lam_im * dt)
    nr, ni = lb_re - 1.0, lb_im
    den = jnp.square(lam_re) + jnp.square(lam_im)
    c_re = (nr * lam_re + ni * lam_im) / den
    c_im = (ni * lam_re - nr * lam_im) / den
    b_re = c_re * bu_re - c_im * bu_im
    b_im = c_re * bu_im + c_im * bu_re
    a_re = jnp.broadcast_to(lb_re, b_re.shape)
    a_im = jnp.broadcast_to(lb_im, b_re.shape)

    def combine(x, y):
        a1r, a1i, b1r, b1i = x
        a2r, a2i, b2r, b2i = y
        return (a2r * a1r - a2i * a1i, a2r * a1i + a2i * a1r,
                a2r * b1r - a2i * b1i + b2r, a2r * b1i + a2i * b1r + b2i)

    _, _, s_re, s_im = lax.associative_scan(combine, (a_re, a_im, b_re, b_im), reverse=reverse, axis=1)
    return s_re, s_im


def _s5_mixer(u, p, i):
    bsz, T, _ = u.shape
    f32 = jnp.float32
    ug = u.astype(f32).reshape(bsz, T, S5_GROUPS, S5_GROUP)
    bu_re = jnp.einsum('btgc,gpc->btgp', ug, p['s5_b_re'][i].astype(f32))
    bu_im = jnp.einsum('btgc,gpc->btgp', ug, p['s5_b_im'][i].astype(f32))
    y = u.astype(f32) * p['s5_d'][i].astype(f32)
    for sfx, rev in (('f', False), ('b', True)):
        s_re, s_im = _s5_scan(bu_re, bu_im, p['s5_lam_re_' + sfx][i], p['s5_lam_im_' + sfx][i],
                              p['s5_log_dt_' + sfx][i], rev)
        yc = (jnp.einsum('btgp,gcp->btgc', s_re, p['s5_c_re_' + sfx][i].astype(f32))
              - jnp.einsum('btgp,gcp->btgc', s_im, p['s5_c_im_' + sfx][i].astype(f32)))
        y = y + yc.reshape(bsz, T, D_MODEL)
    z = jax.nn.gelu(y)
    z = z * jax.nn.sigmoid(z @ p['s5_glu_w'][i].astype(f32) + p['s5_glu_b'][i].astype(f32))
    return z.astype(u.dtype)


def _trunk(x, p):
    for layer in range(DEPTH):
        i = layer // 2
        hn = _rms_norm(x, p['mix_norm'][layer])
        if layer % 2 == 0:
            x = x + _hybrid_mixer(hn, p, i)
        else:
            x = x + _s5_mixer(hn, p, i)
        x = x + _sqrelu_mlp(_rms_norm(x, p['ffn_norm'][layer]), p['ffn_up'][layer], p['ffn_down'][layer])
    return x


def setup_inputs(seed: int = 0) -> dict:
    key = jax.random.key(seed)
    ks = iter(jax.random.split(key, 48))

    def nrm(shape, scale):
        return scale * jax.random.normal(next(ks), shape, jnp.float32)

    def unif(shape, lo, hi):
        return jax.random.uniform(next(ks), shape, jnp.float32, lo, hi)

    E, O, L = N_EVEN, N_ODD, DEPTH
    G, P, GC = S5_GROUPS, S5_STATE, S5_GROUP
    n_idx = jnp.arange(P, dtype=jnp.float32)
    return {
        'x_prompt': nrm((BATCH, SEQ, D_MODEL), 1.0),
        'x_sample': nrm((DEC_BATCH, DEC_SEQ, D_MODEL), 1.0),
        'mix_norm': 1.0 + nrm((L, D_MODEL), 0.02),
        'ffn_norm': 1.0 + nrm((L, D_MODEL), 0.02),
        'ffn_up': nrm((L, D_MODEL, D_FF), D_MODEL ** -0.5),
        'ffn_down': nrm((L, D_FF, D_MODEL), D_FF ** -0.5),
        'hyb_w_in': nrm((E, D_MODEL, IN_WIDTH), D_MODEL ** -0.5),
        'hyb_shift_mu': unif((E, RWKV_IN_WIDTH), 0.0, 1.0),
        'rwkv_w0_f': unif((E, RWKV_DIM), -6.0, 1.0),
        'rwkv_w_up_f': nrm((E, DECAY_LORA, RWKV_DIM), 0.1),
        'rwkv_w0_b': unif((E, RWKV_DIM), -6.0, 1.0),
        'rwkv_w_up_b': nrm((E, DECAY_LORA, RWKV_DIM), 0.1),
        'rwkv_a0': nrm((E, RWKV_DIM), 0.1),
        'rwkv_a_up': nrm((E, ICLR_LORA, RWKV_DIM), ICLR_LORA ** -0.5),
        'rwkv_g_up': nrm((E, GATE_LORA, RWKV_DIM), GATE_LORA ** -0.5),
        'rwkv_k_k': 0.85 + nrm((E, RWKV_DIM), 0.02),
        'rwkv_k_a': 1.0 + nrm((E, RWKV_DIM), 0.02),
        'rwkv_r_k': nrm((E, RWKV_HEADS, RWKV_HEAD), 0.1),
        'rwkv_lnx_g': 1.0 + nrm((E, RWKV_DIM), 0.02),
        'rwkv_lnx_b': nrm((E, RWKV_DIM), 0.02),
        'att_q_norm': 1.0 + nrm((E, HEAD_DIM), 0.02),
        'att_k_norm': 1.0 + nrm((E, HEAD_DIM), 0.02),
        'hyb_w_out': nrm((E, D_MODEL, D_MODEL), D_MODEL ** -0.5),
        's5_lam_re_f': -0.5 + nrm((O, G, P), 0.01),
        's5_lam_im_f': jnp.pi * n_idx + nrm((O, G, P), 0.01),
        's5_log_dt_f': unif((O, G), float(np.log(1e-3)), float(np.log(1e-1))),
        's5_lam_re_b': -0.5 + nrm((O, G, P), 0.01),
        's5_lam_im_b': jnp.pi * n_idx + nrm((O, G, P), 0.01),
        's5_log_dt_b': unif((O, G), float(np.log(1e-3)), float(np.log(1e-1))),
        's5_b_re': nrm((O, G, P, GC), (2 * GC) ** -0.5),
        's5_b_im': nrm((O, G, P, GC), (2 * GC) ** -0.5),
        's5_c_re_f': nrm((O, G, GC, P), P ** -0.5),
        's5_c_im_f': nrm((O, G, GC, P), P ** -0.5),
        's5_c_re_b': nrm((O, G, GC, P), P ** -0.5),
        's5_c_im_b': nrm((O, G, GC, P), P ** -0.5),
        's5_d': nrm((O, D_MODEL), 1.0),
        's5_glu_w': nrm((O, D_MODEL, D_MODEL), D_MODEL ** -0.5),
        's5_glu_b': nrm((O, D_MODEL), 0.02),
    }


def reference(x_prompt, x_sample, mix_norm, ffn_norm, ffn_up, ffn_down, hyb_w_in, hyb_shift_mu,
              rwkv_w0_f, rwkv_w_up_f, rwkv_w0_b, rwkv_w_up_b, rwkv_a0, rwkv_a_up, rwkv_g_up,
              rwkv_k_k, rwkv_k_a, rwkv_r_k, rwkv_lnx_g, rwkv_lnx_b, att_q_norm, att_k_norm, hyb_w_out,
              s5_lam_re_f, s5_lam_im_f, s5_log_dt_f, s5_lam_re_b, s5_lam_im_b, s5_log_dt_b,
              s5_b_re, s5_b_im, s5_c_re_f, s5_c_im_f, s5_c_re_b, s5_c_im_b, s5_d, s5_glu_w, s5_glu_b):
    p = dict(mix_norm=mix_norm, ffn_norm=ffn_norm, ffn_up=ffn_up, ffn_down=ffn_down,
             hyb_w_in=hyb_w_in, hyb_shift_mu=hyb_shift_mu,
             rwkv_w0_f=rwkv_w0_f, rwkv_w_up_f=rwkv_w_up_f, rwkv_w0_b=rwkv_w0_b, rwkv_w_up_b=rwkv_w_up_b,
             rwkv_a0=rwkv_a0, rwkv_a_up=rwkv_a_up, rwkv_g_up=rwkv_g_up,
             rwkv_k_k=rwkv_k_k, rwkv_k_a=rwkv_k_a, rwkv_r_k=rwkv_r_k,
             rwkv_lnx_g=rwkv_lnx_g, rwkv_lnx_b=rwkv_lnx_b,
             att_q_norm=att_q_norm, att_k_norm=att_k_norm, hyb_w_out=hyb_w_out,
             s5_lam_re_f=s5_lam_re_f, s5_lam_im_f=s5_lam_im_f, s5_log_dt_f=s5_log_dt_f,
             s5_lam_re_b=s5_lam_re_b, s5_lam_im_b=s5_lam_im_b, s5_log_dt_b=s5_log_dt_b,
             s5_b_re=s5_b_re, s5_b_im=s5_b_im,
             s5_c_re_f=s5_c_re_f, s5_c_im_f=s5_c_im_f, s5_c_re_b=s5_c_re_b, s5_c_im_b=s5_c_im_b,
             s5_d=s5_d, s5_glu_w=s5_glu_w, s5_glu_b=s5_glu_b)
    y_prompt = _trunk(x_prompt, p)
    y_sample = _trunk(x_sample, p)
    return (y_prompt, y_sample)
```

```python
import math
import numpy as np
import concourse.bass as bass
import concourse.mybir as mybir
from concourse.bass_utils import run_bass_kernel_spmd

F32 = mybir.dt.float32
BF16 = mybir.dt.bfloat16
AF = mybir.ActivationFunctionType
ALU = mybir.AluOpType
AX = mybir.AxisListType

D = 1024
DFF = 4096
NCORES = 8
SAME_ENGINE_SYNC = True


class Dep:
    __slots__ = ("w", "r")

    def __init__(self):
        self.w = None
        self.r = {}


class V:
    __slots__ = ("ap", "dep")

    def __init__(self, ap, dep):
        self.ap = ap
        self.dep = dep


class Buf:
    def __init__(self, k, name, ap):
        self.k = k
        self.name = name
        self.ap = ap
        self.deps = {}

    def d(self, key=None):
        if key not in self.deps:
            self.deps[key] = Dep()
        return self.deps[key]

    def __getitem__(self, idx):
        return V(self.ap[idx], self.d(None))

    def v(self, ap, key=None):
        return V(ap, self.d(key))

    def key(self, key):
        b = self

        class _K:
            def __getitem__(s, idx):
                return V(b.ap[idx], b.d(key))
        return _K()


class Eng:
    def __init__(self, name, sem):
        self.name = name
        self.sem = sem
        self.cnt = 0
        self.seen = {}
        self.ops = []
        self.dsems = []
        self.dma_n = 0


class K:
    def __init__(self, nc, stack):
        self.nc = nc
        self.stack = stack
        self.eng = {}
        for n in ("pe", "act", "dve", "pool", "sp"):
            sem = stack.enter_context(nc.semaphore("s_" + n))
            self.eng[n] = Eng(n, sem)
        for n in ("sp", "pool", "act"):
            for i in range(8):
                self.eng[n].dsems.append(stack.enter_context(nc.semaphore("d_%s%d" % (n, i))))
        self.allsems = []
        self.nins = 0

    def _wait(self, e, ev):
        if ev is None:
            return
        src, sem, val = ev
        if src is e and not (SAME_ENGINE_SYNC and e.name in ("act", "dve", "pool")):
            return
        key = id(sem)
        if e.seen.get(key, 0) >= val:
            return
        e.seen[key] = val
        e.ops.append(("w", sem, val))

    def _deps(self, e, reads, writes):
        for v in reads:
            self._wait(e, v.dep.w)
        for v in writes:
            self._wait(e, v.dep.w)
            for o, ev in v.dep.r.items():
                if o != e.name or e.name != "pe":
                    self._wait(e, ev)

    def _mark(self, e, ev, reads, writes):
        for v in writes:
            v.dep.w = ev
            v.dep.r = {}
        for v in reads:
            v.dep.r[e.name] = ev

    def op(self, en, fn, reads, writes):
        e = self.eng[en]
        reads = [v for v in reads if isinstance(v, V)]
        self._deps(e, reads, writes)
        e.cnt += 1
        ev = (e, e.sem, e.cnt)
        e.ops.append(("i", fn, e.sem, 1))
        self._mark(e, ev, reads, writes)
        self.nins += 1

    def dma(self, out, in_, q="sp", slow=False):
        e = self.eng[q]
        n = e.dma_n
        e.dma_n += 1
        P = len(e.dsems)
        s = e.dsems[n % P]
        if n >= P:
            self._wait(e, (None, s, 16 * (n // P)))
        self._deps(e, [in_], [out])
        ev = (None, s, 16 * (n // P + 1))
        oa, ia = out.ap, in_.ap
        if slow:
            e.ops.append(("i", lambda h: h.dma_start(out=oa, in_=ia, allow_slow_non_contiguous=True), s, 16))
        else:
            e.ops.append(("i", lambda h: h.dma_start(out=oa, in_=ia), s, 16))
        out.dep.w = ev
        out.dep.r = {}
        in_.dep.r[("dma", q, n % P)] = ev
        self.nins += 1

    def barrier(self):
        evs = []
        for n, e in self.eng.items():
            if e.cnt:
                evs.append((e, e.sem, e.cnt))
            P = len(e.dsems)
            for i in range(min(P, e.dma_n)):
                cnt = (e.dma_n - 1 - i) // P + 1
                evs.append((None, e.dsems[i], 16 * cnt))
        for n, e in self.eng.items():
            for ev in evs:
                if ev[0] is e:
                    continue
                self._wait(e, ev)

    def replay(self, block):
        def mk(e):
            def f(h):
                for o in e.ops:
                    if o[0] == "w":
                        h.wait_ge(o[1], o[2])
                    else:
                        o[1](h).then_inc(o[2], o[3])
            return f
        block.tensor(mk(self.eng["pe"]))
        block.scalar(mk(self.eng["act"]))
        block.vector(mk(self.eng["dve"]))
        block.gpsimd(mk(self.eng["pool"]))
        block.sync(mk(self.eng["sp"]))

    def mm(self, out, lhsT, rhs, start=True, stop=True):
        o, l, r = out.ap, lhsT.ap, rhs.ap
        self.op("pe", lambda h: h.matmul(o, lhsT=l, rhs=r, start=start, stop=stop), [lhsT, rhs], [out])

    def tr(self, out, in_, ident):
        o, i, d = out.ap, in_.ap, ident.ap
        self.op("pe", lambda h: h.transpose(o, i, d), [in_, ident], [out])

    def act(self, out, in_, func, bias=None, scale=1.0, accum=None):
        o, i = out.ap, in_.ap
        kw = {}
        rd = [in_]
        wr = [out]
        if bias is not None:
            kw["bias"] = bias.ap if isinstance(bias, V) else bias
            rd.append(bias)
        if isinstance(scale, V):
            kw["scale"] = scale.ap
            rd.append(scale)
        else:
            kw["scale"] = scale
        if accum is not None:
            kw["accum_out"] = accum.ap
            wr.append(accum)
        self.op("act", lambda h: h.activation(out=o, in_=i, func=func, **kw), rd, wr)

    def tt(self, en, out, in0, in1, op):
        o, a, b = out.ap, in0.ap, in1.ap
        self.op(en, lambda h: h.tensor_tensor(out=o, in0=a, in1=b, op=op), [in0, in1], [out])

    def ts(self, en, out, in0, s1, s2, op0, op1=None, accum=None):
        o, a = out.ap, in0.ap
        x1 = s1.ap if isinstance(s1, V) else s1
        x2 = s2.ap if isinstance(s2, V) else s2
        kw = {}
        wr = [out]
        if op1 is not None:
            kw["op1"] = op1
        if accum is not None:
            kw["accum_out"] = accum.ap
            wr.append(accum)
        self.op(en, lambda h: h.tensor_scalar(out=o, in0=a, scalar1=x1, scalar2=x2, op0=op0, **kw),
                [in0, s1, s2], wr)

    def stt(self, en, out, in0, s, in1, op0, op1):
        o, a, b = out.ap, in0.ap, in1.ap
        x = s.ap if isinstance(s, V) else s
        self.op(en, lambda h: h.scalar_tensor_tensor(out=o, in0=a, scalar=x, in1=b, op0=op0, op1=op1),
                [in0, s, in1], [out])

    def copy(self, en, out, in_):
        o, i = out.ap, in_.ap
        if en == "act":
            self.op("act", lambda h: h.copy(out=o, in_=i), [in_], [out])
        else:
            self.op(en, lambda h: h.tensor_copy(out=o, in_=i), [in_], [out])

    def memset(self, en, out, val):
        o = out.ap
        self.op(en, lambda h: h.memset(o, val), [], [out])

    def scan(self, out, d0, d1, init, op0=ALU.mult, op1=ALU.add):
        o, a, b = out.ap, d0.ap, d1.ap
        x = init.ap if isinstance(init, V) else init
        self.op("dve", lambda h: h.tensor_tensor_scan(out=o, data0=a, data1=b, initial=x, op0=op0, op1=op1),
                [d0, d1, init], [out])

    def reduce(self, out, in_, op=ALU.add, axis=AX.X):
        o, i = out.ap, in_.ap
        self.op("dve", lambda h: h.tensor_reduce(out=o, in_=i, axis=axis, op=op), [in_], [out])

    def recip(self, out, in_):
        o, i = out.ap, in_.ap
        self.op("dve", lambda h: h.reciprocal(out=o, in_=i), [in_], [out])


RW = 1856
INW = 2624
EPS = 1e-6


class Arena:
    def __init__(self, k, ap_f32, ncols):
        self.k, self.ap, self.n, self.off, self.cnt = k, ap_f32, ncols, 0, 0

    def reset(self):
        self.off = 0

    def f32(self, name, cols):
        a = self.ap[:, self.off:self.off + cols]
        self.off += cols
        assert self.off <= self.n, (name, self.off)
        self.cnt += 1
        return Buf(self.k, "%s_%d" % (name, self.cnt), a)

    def bf16(self, name, cols):
        c32 = (cols + 1) // 2
        a = self.ap[:, self.off:self.off + c32].bitcast(BF16)[:, 0:cols]
        self.off += c32
        assert self.off <= self.n, (name, self.off)
        self.cnt += 1
        return Buf(self.k, "%s_%d" % (name, self.cnt), a)


def r3(ap, b):
    return ap.rearrange("p (a b) -> p a b", b=b)


def build(T, stage=99):
    from contextlib import ExitStack
    NT = T // 128
    nc = bass.Bass("TRN2", target_bir_lowering=False)
    st = ExitStack()
    k = K(nc, st)

    def din(name, shape, dt=F32):
        return Buf(k, name, nc.dram_tensor(name, list(shape), dt, kind="ExternalInput").ap())

    def dscr(name, shape, dt=F32):
        return Buf(k, name, nc.dram_tensor(name, list(shape), dt, kind="Internal").ap())

    x_in = din("x", [T, D])
    mix_norm = din("mix_norm", [2, D]); ffn_norm = din("ffn_norm", [2, D])
    ffn_up = din("ffn_up", [2, D, DFF]); ffn_down = din("ffn_down", [2, DFF, D])
    w_in = din("hyb_w_in", [D, INW]); mu = din("hyb_shift_mu", [1, RW])
    w_out = din("hyb_w_out", [D, D])
    identb_d = din("identb", [128, 128]);
    y_out = Buf(k, "y", nc.dram_tensor("y", [T, D], F32, kind="ExternalOutput").ap())
    hs_d = dscr("hs", [T, INW])
    x1_d = dscr("x1", [T, D])

    arena_t = st.enter_context(nc.sbuf_tensor("arena", [128, 51200], F32))
    A = Arena(k, arena_t, 51200)
    PS = [Buf(k, "ps%d" % i, st.enter_context(nc.psum_tensor("ps%d" % i, [128, 512], F32))) for i in range(8)]

    ident_f = A.f32("identf", 128)
    ident_b = A.bf16("identb", 128)
    k.dma(ident_f[:, :], identb_d[:, :])
    k.copy("dve", ident_b[:, :], ident_f[:, :])
    gT = A.f32("gT", 32)
    for i, src in enumerate((mix_norm, mix_norm, ffn_norm, ffn_norm)):
        row = src.ap[i % 2:i % 2 + 1, :].rearrange("o (c p) -> p (o c)", p=128)
        k.dma(gT.v(gT.ap[:, i * 8:(i + 1) * 8]), src.v(row), slow=True)
    const_end = A.off

    def norm_T(xt, gi, out3, tmp):
        sq, ss, xb, pst = tmp
        k.act(sq[:, :], xt, AF.Square, accum=ss[:, 0:1])
        k.ts("dve", ss[:, 1:2], ss[:, 0:1], 1.0 / D, EPS, ALU.mult, ALU.add)
        k.act(ss[:, 2:3], ss[:, 1:2], AF.Sqrt)
        k.recip(ss[:, 3:4], ss[:, 2:3])
        k.ts("dve", xb[:, :], xt, ss[:, 3:4], None, ALU.mult)
        pb = pst.ap.bitcast(BF16)
        for c in range(8):
            k.tr(pst.v(pb[:, c * 128:(c + 1) * 128]), xb[:, c * 128:(c + 1) * 128], ident_b[:, :])
        g3 = gT.ap[:, gi * 8:(gi + 1) * 8].rearrange("p (a o) -> p a o", o=1).to_broadcast([128, 8, 128])
        k.tt("dve", out3, pst.v(r3(pb[:, 0:1024], 128)), gT.v(g3), ALU.mult)

    def load_w_bf16(dst3, src_ap3, stage_buf, nchunk, cols, eng="pool"):
        for c in range(nchunk):
            k.dma(stage_buf[:, 0:cols], src_ap3(c))
            k.copy(eng, dst3(c), stage_buf[:, 0:cols])

    def ffn_phase(layer, src_d, dst_d, pre_alloc=None, pre=None):
        A.off = const_end
        pre_state = pre_alloc() if pre_alloc else None
        wu = A.bf16("wu", 8 * DFF); wd = A.bf16("wd", 32 * D)
        stg = [A.f32("stg0", 1024)] * 2
        for c in range(32):
            s = stg[c % 2]
            k.dma(s[:, 0:1024], ffn_up.v(ffn_up.ap[layer, (c // 4) * 128:(c // 4 + 1) * 128, (c % 4) * 1024:(c % 4 + 1) * 1024]))
            k.copy("act" if c % 2 else "dve", wu[:, c * 1024:(c + 1) * 1024], s[:, 0:1024])
        for c in range(32):
            s = stg[c % 2]
            k.dma(s[:, 0:1024], ffn_down.v(ffn_down.ap[layer, c * 128:(c + 1) * 128, :]))
            k.copy("act" if c % 2 else "dve", wd[:, c * 1024:(c + 1) * 1024], s[:, 0:1024])
        NB = 2
        xt = [A.f32("xt", D) for _ in range(NB)]
        ss = [A.f32("ss", 4) for _ in range(NB)]
        xb = [A.bf16("xb", D) for _ in range(NB)]
        xT = [A.bf16("xT", D) for _ in range(NB)]
        h = [A.bf16("h", DFF)] * NB
        sq = A.bf16("sqj", D)
        hT = [A.bf16("hT", DFF)] * NB
        yo = xt
        def load_norm(i):
            b = i % NB
            k.dma(xt[b][:, :], src_d[i * 128:(i + 1) * 128, :])
            if pre:
                pre(pre_state, i, xt[b])
            norm_T(xt[b][:, :], 2 + layer, xT[b].v(r3(xT[b].ap, 128)), (sq, ss[b], xb[b], PS[0]))

        def up(b, q):
            for j in (2 * q, 2 * q + 1):
                ps = PS[1 + j % 3]
                for c in range(8):
                    k.mm(ps[:, :], xT[b][:, c * 128:(c + 1) * 128], wu[:, c * DFF + j * 512:c * DFF + (j + 1) * 512],
                         start=(c == 0), stop=(c == 7))
                hj = h[b].key(j)[:, j * 512:(j + 1) * 512]
                k.act(hj, ps[:, :], AF.Relu)
                if j % 2 == 0:
                    k.tt("dve", hj, hj, hj, ALU.mult)
                else:
                    k.act(hj, hj, AF.Square)

        def tr(b, q):
            ps = PS[4 + q % 2]
            pb = ps.ap.bitcast(BF16)
            for c in range(8):
                cc = q * 8 + c
                k.tr(ps.v(pb[:, c * 128:(c + 1) * 128]), h[b].key(cc // 4)[:, cc * 128:(cc + 1) * 128], ident_b[:, :])
            k.copy("dve" if q % 2 == 0 else "act", hT[b].key(q)[:, q * 1024:(q + 1) * 1024], ps.v(pb[:, 0:1024]))

        def down(b, q):
            for c in range(q * 8, q * 8 + 8):
                for j in range(2):
                    k.mm(PS[6 + j][:, :], hT[b].key(q)[:, c * 128:(c + 1) * 128], wd[:, c * D + j * 512:c * D + (j + 1) * 512],
                         start=(c == 0), stop=(c == 31))

        load_norm(0)
        up(0, 0)
        up(0, 1)
        for i in range(NT):
            b = i % NB
            nb = (i + 1) % NB
            if i + 1 < NT:
                load_norm(i + 1)
            tr(b, 0)
            up(b, 2)
            down(b, 0)
            tr(b, 1)
            up(b, 3)
            down(b, 1)
            tr(b, 2)
            if i + 1 < NT:
                up(nb, 0)
            down(b, 2)
            tr(b, 3)
            if i + 1 < NT:
                up(nb, 1)
            down(b, 3)
            for j in range(2):
                k.tt("dve", yo[b][:, j * 512:(j + 1) * 512], PS[6 + j][:, :], xt[b][:, j * 512:(j + 1) * 512], ALU.add)
            k.dma(dst_d[i * 128:(i + 1) * 128, :], yo[b][:, :])
        k.barrier()


    hnT_d = dscr("hnT", [128, 8, T], BF16)
    qT_d = dscr("qT", [64, 8, T], BF16)
    kT_d = dscr("kT", [64, 2, T], BF16)
    v_d = dscr("v_att", [T, 128], BF16)
    ybT_d = Buf(k, "ybT", nc.dram_tensor("ybT", [64, 8, T], BF16, kind="ExternalOutput" if stage in (1, 11, 12) else "Internal").ap())
    rope_d = din("rope", [T, 64])
    qn_d = din("att_q_norm", [1, 64]); kn_d = din("att_k_norm", [1, 64])

    def phase_norm(src_d, gi):
        A.off = const_end
        NB = 2
        xt = [A.f32("xt", D) for _ in range(NB)]
        sq = A.f32("sq", D); ss = [A.f32("ss", 4) for _ in range(NB)]
        xb = [A.bf16("xb", D) for _ in range(NB)]
        xT = [A.bf16("xT", D) for _ in range(NB)]
        for i in range(NT):
            b = i % NB
            k.dma(xt[b][:, :], src_d[i * 128:(i + 1) * 128, :])
            norm_T(xt[b][:, :], gi, xT[b].v(r3(xT[b].ap, 128)), (sq, ss[b], xb[b], PS[i % 2]))
            k.dma(hnT_d[:, :, i * 128:(i + 1) * 128], xT[b].v(r3(xT[b].ap, 128)))
        k.barrier()

    def phase_inproj():
        A.off = const_end
        W1 = A.bf16("W1", 8 * RW); W2 = A.bf16("W2", 8 * RW); WA = A.bf16("WA", 8 * 768)
        stg = A.f32("stg", INW); mu_b = A.f32("mub", RW); mu_h = A.f32("muh", RW); tmpw = A.f32("tmpw", RW)
        k.dma(mu_b[:, :], mu.v(mu.ap[0:1, :].to_broadcast([128, RW])))
        k.ts("dve", mu_h[:, :], mu_b[:, :], 0.5, None, ALU.mult)
        k.ts("dve", mu_b[:, :], mu_b[:, :], -1.0, 1.0, ALU.mult, ALU.add)
        for c in range(8):
            k.dma(stg[:, :], w_in[c * 128:(c + 1) * 128, :])
            k.tt("dve", W1[:, c * RW:(c + 1) * RW], stg[:, 0:RW], mu_b[:, :], ALU.mult)
            k.tt("dve", W2[:, c * RW:(c + 1) * RW], stg[:, 0:RW], mu_h[:, :], ALU.mult)
            k.copy("act", WA[:, c * 768:(c + 1) * 768], stg[:, RW:INW])
        G = A.f32("G", 640)
        k.dma(G.v(r3(G.ap[:, 0:512], 64)), qn_d.v(qn_d.ap[0:1, :].unsqueeze(1).to_broadcast([128, 8, 64])))
        k.dma(G.v(r3(G.ap[:, 512:640], 64)), kn_d.v(kn_d.ap[0:1, :].unsqueeze(1).to_broadcast([128, 2, 64])))
        k.ts("dve", G[:, 0:512], G[:, 0:512], 0.125, None, ALU.mult)
        import os
        DBG = int(os.environ.get("DBG", "99"))
        if DBG == 0:
            k.barrier(); return
        NB = 2
        ext = [A.bf16("ext", 8 * 130) for _ in range(NB)]
        avg = [A.bf16("avg", 8 * 128) for _ in range(NB)]
        hs = [A.f32("hs", RW) for _ in range(NB)]
        qk = [A.f32("qk", 640) for _ in range(NB)]
        qsq = A.f32("qsq", 640)
        st10 = [A.f32("st10", 40) for _ in range(NB)]
        rp = [A.f32("rp", 64) for _ in range(NB)]
        t1 = A.f32("t1", 320); t2 = A.f32("t2", 320)
        qkb = [A.bf16("qkb", 640) for _ in range(NB)]
        vb = [A.bf16("vb", 128) for _ in range(NB)]
        qkT = [A.bf16("qkT", 1280) for _ in range(NB)]
        groups = [(0, 512), (512, 512), (1024, 512), (1536, 320)]
        for i in range(NT):
            b = i % NB
            e3 = r3(ext[b].ap, 130)
            lo = 1 if i == 0 else 0
            hi = 129 if i == NT - 1 else 130
            if i == 0:
                k.memset("pool", ext[b].v(e3[:, :, 0:1]), 0.0)
            if i == NT - 1:
                k.memset("pool", ext[b].v(e3[:, :, 129:130]), 0.0)
            k.dma(ext[b].v(e3[:, :, lo:hi]), hnT_d[:, :, i * 128 - 1 + lo:i * 128 - 1 + hi])
            a3 = r3(avg[b].ap, 128)
            k.tt("dve", avg[b].v(a3), ext[b].v(e3[:, :, 0:128]), ext[b].v(e3[:, :, 2:130]), ALU.add)
            for gi_, (c0, cw) in enumerate(groups):
                ps = PS[gi_ % 4]
                for c in range(8):
                    k.mm(ps[:, 0:cw], ext[b].v(e3[:, c, 1:129]), W1[:, c * RW + c0:c * RW + c0 + cw], start=(c == 0), stop=False)
                for c in range(8):
                    k.mm(ps[:, 0:cw], avg[b].v(a3[:, c, :]), W2[:, c * RW + c0:c * RW + c0 + cw], start=False, stop=(c == 7))
                k.copy("act", hs[b][:, c0:c0 + cw], ps[:, 0:cw])
            if DBG == 1:
                continue
            if DBG != 21:
                k.dma(hs_d[i * 128:(i + 1) * 128, 0:RW], hs[b][:, :])
            for gi_, (c0, cw) in enumerate(((0, 512), (512, 256))):
                ps = PS[4 + gi_]
                for c in range(8):
                    k.mm(ps[:, 0:cw], ext[b].v(e3[:, c, 1:129]), WA[:, c * 768 + c0:c * 768 + c0 + cw], start=(c == 0), stop=(c == 7))
            k.copy("act", qk[b][:, 0:512], PS[4][:, 0:512])
            k.copy("act", qk[b][:, 512:640], PS[5][:, 0:128])
            k.copy("act", vb[b][:, :], PS[5][:, 128:256])
            if DBG != 22:
                k.dma(v_d[i * 128:(i + 1) * 128, :], vb[b][:, :])
            if DBG in (2, 21, 22):
                continue
            q3 = r3(qk[b].ap, 64)
            k.act(qsq[:, :], qk[b][:, :], AF.Square)
            k.reduce(st10[b][:, 0:10], qsq.v(r3(qsq.ap, 64)))
            k.ts("dve", st10[b][:, 10:20], st10[b][:, 0:10], 1.0 / 64, EPS, ALU.mult, ALU.add)
            k.act(st10[b][:, 20:30], st10[b][:, 10:20], AF.Sqrt)
            k.recip(st10[b][:, 30:40], st10[b][:, 20:30])
            rb = st10[b].ap[:, 30:40].rearrange("p (a o) -> p a o", o=1).to_broadcast([128, 10, 64])
            k.tt("dve", qk[b].v(q3), qk[b].v(q3), st10[b].v(rb), ALU.mult)
            k.tt("dve", qk[b][:, :], qk[b][:, :], G[:, :], ALU.mult)
            if DBG == 3:
                continue
            k.dma(rp[b][:, :], rope_d[i * 128:(i + 1) * 128, :])
            q4 = qk[b].ap.rearrange("p (h i two) -> p h i two", i=32, two=2)
            x1 = qk[b].v(q4[:, :, :, 0]); x2 = qk[b].v(q4[:, :, :, 1])
            cb = rp[b].v(rp[b].ap[:, 0:32].unsqueeze(1).to_broadcast([128, 10, 32]))
            sb = rp[b].v(rp[b].ap[:, 32:64].unsqueeze(1).to_broadcast([128, 10, 32]))
            o4 = qkb[b].ap.rearrange("p (h i two) -> p h i two", i=32, two=2)
            t13 = t1.v(r3(t1.ap, 32)); t23 = t2.v(r3(t2.ap, 32))
            k.tt("dve", t13, x1, cb, ALU.mult)
            k.tt("dve", t23, x2, sb, ALU.mult)
            k.tt("dve", qkb[b].v(o4[:, :, :, 0]), t13, t23, ALU.subtract)
            k.tt("dve", t13, x1, sb, ALU.mult)
            k.tt("dve", t23, x2, cb, ALU.mult)
            k.tt("dve", qkb[b].v(o4[:, :, :, 1]), t13, t23, ALU.add)
            if DBG == 4:
                continue
            for half in range(2):
                ps = PS[6 + half]
                pb = ps.ap.bitcast(BF16)
                for hh in range(5):
                    h_ = half * 5 + hh
                    k.tr(ps.v(pb[0:64, hh * 128:(hh + 1) * 128]), qkb[b][:, h_ * 64:(h_ + 1) * 64], ident_b[:, :])
                k.copy("act" if half else "dve", qkT[b][0:64, half * 640:(half + 1) * 640], ps.v(pb[0:64, 0:640]))
            t3 = r3(qkT[b].ap, 128)
            k.dma(qT_d[:, :, i * 128:(i + 1) * 128], qkT[b].v(t3[0:64, 0:8, :]))
            k.dma(kT_d[:, :, i * 128:(i + 1) * 128], qkT[b].v(t3[0:64, 8:10, :]))
        k.barrier()

    def phase_attn():
        A.off = const_end
        kT = A.bf16("kT", T)
        vx = A.bf16("vx", NT * 2 * 65)
        for g_ in range(2):
            k.dma(kT[g_ * 64:(g_ + 1) * 64, :], kT_d[:, g_, :])
        v4 = vx.ap.rearrange("p (n g d) -> p n g d", g=2, d=65)
        k.memset("dve", vx.v(v4[:, :, :, 64:65]), 1.0)
        for g_ in range(2):
            k.dma(vx.v(v4[:, :, g_, 0:64]), v_d.v(v_d.ap[:, g_ * 64:(g_ + 1) * 64].rearrange("(n p) d -> p n d", p=128)))
        sel = A.f32("sel", 64)
        k.memset("dve", sel[:, :], 0.0)
        k.memset("dve", sel[64:65, :], 1.0)
        NQ = 3
        qt = [A.bf16("qt", 512) for _ in range(NQ)]
        NP = 3
        pT = [[A.bf16("pT", 512) for _ in range(NP)] for _ in range(2)]
        ot = [A.f32("ot", 512) for _ in range(2)]
        rinv = [A.f32("rinv", 512) for _ in range(2)]
        yb = [A.bf16("yb", 512) for _ in range(2)]
        flat = [(i_, j_) for i_ in range(NT) for j_ in range(NT)]
        nxt = [0]
        sc = [0]
        pend = []
        AHEAD = 2

        def issue_next():
            if nxt[0] >= len(flat):
                return
            i_, j_ = flat[nxt[0]]
            nxt[0] += 1
            qb = qt[i_ % NQ]
            if j_ == 0:
                for g_ in range(2):
                    k.dma(qb.v(r3(qb.ap, 128)[g_ * 64:(g_ + 1) * 64, :, :]), qT_d[:, g_ * 4:(g_ + 1) * 4, i_ * 128:(i_ + 1) * 128])
            pair = []
            for g_ in range(2):
                ps = PS[(sc[0] % 3) * 2 + g_]
                k.mm(ps[:, :], kT[g_ * 64:(g_ + 1) * 64, j_ * 128:(j_ + 1) * 128], qb[g_ * 64:(g_ + 1) * 64, :])
                pair.append(ps)
            sc[0] += 1
            pend.append(pair)

        for _ in range(AHEAD):
            issue_next()
        it = 0
        for i in range(NT):
            po = [PS[6], PS[7]]
            for j in range(NT):
                pair = pend.pop(0)
                for g_ in range(2):
                    p_ = pT[g_][it % NP]
                    k.act(p_[:, :], pair[g_][:, :], AF.Exp)
                issue_next()
                for g_ in range(2):
                    p_ = pT[g_][it % NP]
                    k.mm(po[g_][0:65, :], vx.v(v4[:, j, g_, :]), p_[:, :], start=(j == 0), stop=(j == NT - 1))
                it += 1
            for g_ in range(2):
                k.copy("dve", ot[g_][0:65, :], po[g_][0:65, :])
                pbc = po[g_]
                k.mm(pbc[0:64, :], sel[0:65, :], ot[g_][0:65, :])
                k.recip(rinv[g_][0:64, :], pbc[0:64, :])
                k.tt("dve", yb[g_][0:64, :], ot[g_][0:64, :], rinv[g_][0:64, :], ALU.mult)
                k.dma(ybT_d[:, g_ * 4:(g_ + 1) * 4, i * 128:(i + 1) * 128], yb[g_].v(r3(yb[g_].ap, 128)[0:64, :, :]))
        k.barrier()

    if stage in (1, 11, 12):
        phase_norm(x_in, 0)
        if stage != 12:
            phase_inproj()
        if stage == 1:
            phase_attn()


    s5in = {}
    for nm, shp in (("s5_lam_re_f", [64, 64]), ("s5_lam_im_f", [64, 64]), ("s5_log_dt_f", [1, 64]),
                    ("s5_lam_re_b", [64, 64]), ("s5_lam_im_b", [64, 64]), ("s5_log_dt_b", [1, 64]),
                    ("s5_b_re", [64, 64, 16]), ("s5_b_im", [64, 64, 16]),
                    ("s5_c_re_f", [64, 16, 64]), ("s5_c_im_f", [64, 16, 64]),
                    ("s5_c_re_b", [64, 16, 64]), ("s5_c_im_b", [64, 16, 64]),
                    ("s5_d", [1, D]), ("s5_glu_w", [D, D]), ("s5_glu_b", [1, D]), ("rowmask", [128, 8])):
        s5in[nm] = din(nm, shp)
    yS_d = dscr("yS", [128, 8, T])
    m_d = Buf(k, "m_s5", nc.dram_tensor("m_s5", [T, D], F32, kind="ExternalOutput" if stage == 2 else "Internal").ap())

    def phase_s5():
        NCH = 32
        for d_, sfx in enumerate(("f", "b")):
            A.off = const_end
            Ere = A.bf16("Ere", 4096); Eim = A.bf16("Eim", 4096)
            Bp = [A.bf16("Bp", 4096) for _ in range(2)]
            Cp = [A.bf16("Cp", 4096) for _ in range(2)]
            rho = A.f32("rho", NCH)
            carry = [A.f32("carry", NCH) for _ in range(2)]
            rmask = A.f32("rmask", 8)
            k.dma(rmask[:, :], s5in["rowmask"][:, :])
            if d_ == 1:
                gw = A.bf16("gw", 8 * D)
                Dd = A.bf16("Dd", 8 * 128)
                gb = A.f32("gb", 8)
                k.dma(gb[:, :], s5in["s5_glu_b"].v(s5in["s5_glu_b"].ap[0:1, :].rearrange("o (c p) -> p (o c)", p=128)), slow=True)
            work0 = A.off
            sm = {n: A.f32(n, NCH) for n in ("lre", "lim", "dt", "th", "c", "s", "c2", "s2", "t", "u", "lbr", "lbi", "nr", "den", "cre", "cim", "rd")}
            stg = A.f32("stg5", 1024)
            if d_ == 1:
                dcol = A.f32("dcol", 8)
                k.dma(dcol[:, :], s5in["s5_d"].v(s5in["s5_d"].ap[0:1, :].rearrange("o (c p) -> p (o c)", p=128)), slow=True)
                for c in range(8):
                    k.ts("dve", Dd[:, c * 128:(c + 1) * 128], ident_f[:, :], dcol[:, c:c + 1], None, ALU.mult)
                for c in range(8):
                    k.dma(stg[:, :], s5in["s5_glu_w"][c * 128:(c + 1) * 128, :])
                    k.copy("dve", gw[:, c * D:(c + 1) * D], stg[:, :])
            Bch = [A.f32("Bch", 512) for _ in range(2)]
            for ri, nm in enumerate(("s5_b_re", "s5_b_im")):
                src = s5in[nm]
                k.dma(Bch[ri].v(r3(Bch[ri].ap, 16)), src.v(src.ap.rearrange("(cc gl) p c -> (gl p) cc c", gl=2)))
            Bh = [A.f32("Bh", 512) for _ in range(2)]
            Bpf = A.f32("Bpf", 4096)
            Cn = A.f32("Cn", 512)
            Cpf = A.f32("Cpf", 4096)
            Eref = A.f32("Eref", 4096); Eimf = A.f32("Eimf", 4096)
            tA = A.f32("tA", 512); tB = A.f32("tB", 512)
            lre, lim = sm["lre"], sm["lim"]
            for dst, nm in ((lre, "s5_lam_re_" + sfx), (lim, "s5_lam_im_" + sfx)):
                src = s5in[nm]
                k.dma(dst[:, :], src.v(src.ap.rearrange("(cc gl) p -> (gl p) cc", gl=2)), slow=True)
            ld = s5in["s5_log_dt_" + sfx]
            for gl in range(2):
                k.dma(sm["dt"][gl * 64:(gl + 1) * 64, :],
                      ld.v(ld.ap[0:1, :].rearrange("o (cc gl) -> o cc gl", gl=2)[:, :, gl].to_broadcast([64, NCH])), slow=True)
            k.act(sm["dt"][:, :], sm["dt"][:, :], AF.Exp)
            k.tt("dve", sm["t"][:, :], lre[:, :], sm["dt"][:, :], ALU.mult)
            k.act(rho[:, :], sm["t"][:, :], AF.Exp)
            k.tt("dve", sm["th"][:, :], lim[:, :], sm["dt"][:, :], ALU.mult)
            k.act(sm["s"][:, :], sm["th"][:, :], AF.Sin, scale=1.0 / 32)
            k.ts("dve", sm["t"][:, :], sm["th"][:, :], 1.0 / 32, math.pi / 2, ALU.mult, ALU.add)
            k.act(sm["c"][:, :], sm["t"][:, :], AF.Sin)
            for _ in range(5):
                k.tt("dve", sm["c2"][:, :], sm["c"][:, :], sm["c"][:, :], ALU.mult)
                k.tt("dve", sm["s2"][:, :], sm["s"][:, :], sm["s"][:, :], ALU.mult)
                k.tt("dve", sm["t"][:, :], sm["c"][:, :], sm["s"][:, :], ALU.mult)
                k.tt("dve", sm["c"][:, :], sm["c2"][:, :], sm["s2"][:, :], ALU.subtract)
                k.ts("dve", sm["s"][:, :], sm["t"][:, :], 2.0, None, ALU.mult)
            k.tt("dve", sm["lbr"][:, :], rho[:, :], sm["c"][:, :], ALU.mult)
            k.tt("dve", sm["lbi"][:, :], rho[:, :], sm["s"][:, :], ALU.mult)
            k.ts("dve", sm["nr"][:, :], sm["lbr"][:, :], -1.0, None, ALU.add)
            k.tt("dve", sm["t"][:, :], lre[:, :], lre[:, :], ALU.mult)
            k.tt("dve", sm["u"][:, :], lim[:, :], lim[:, :], ALU.mult)
            k.tt("dve", sm["den"][:, :], sm["t"][:, :], sm["u"][:, :], ALU.add)
            k.recip(sm["rd"][:, :], sm["den"][:, :])
            k.tt("dve", sm["t"][:, :], sm["nr"][:, :], lre[:, :], ALU.mult)
            k.tt("dve", sm["u"][:, :], sm["lbi"][:, :], lim[:, :], ALU.mult)
            k.tt("dve", sm["t"][:, :], sm["t"][:, :], sm["u"][:, :], ALU.add)
            k.tt("dve", sm["cre"][:, :], sm["t"][:, :], sm["rd"][:, :], ALU.mult)
            k.tt("dve", sm["t"][:, :], sm["lbi"][:, :], lre[:, :], ALU.mult)
            k.tt("dve", sm["u"][:, :], sm["nr"][:, :], lim[:, :], ALU.mult)
            k.tt("dve", sm["t"][:, :], sm["t"][:, :], sm["u"][:, :], ALU.subtract)
            k.tt("dve", sm["cim"][:, :], sm["t"][:, :], sm["rd"][:, :], ALU.mult)
            e3r = r3(Eref.ap, 128); e3i = r3(Eimf.ap, 128)
            c0 = 0 if d_ == 0 else 127
            k.copy("dve", Eref.v(e3r[:, :, c0]), sm["c"][:, :])
            k.ts("dve", Eimf.v(e3i[:, :, c0]), sm["s"][:, :], -1.0, None, ALU.mult)
            n = 1
            while n < 128:
                last = n - 1 if d_ == 0 else 128 - n
                for m0 in range(0, n, 16):
                    w_ = min(16, n)
                    if d_ == 0:
                        ss_ = slice(m0, m0 + w_); dd_ = slice(n + m0, n + m0 + w_)
                    else:
                        ss_ = slice(128 - n + m0, 128 - n + m0 + w_); dd_ = slice(128 - 2 * n + m0, 128 - 2 * n + m0 + w_)
                    cr = Eref.v(e3r[:, :, last:last + 1].to_broadcast([128, NCH, w_]))
                    ci = Eimf.v(e3i[:, :, last:last + 1].to_broadcast([128, NCH, w_]))
                    ta = tA.v(r3(tA.ap, 16)[:, :, 0:w_]); tb = tB.v(r3(tB.ap, 16)[:, :, 0:w_])
                    k.tt("dve", ta, Eref.v(e3r[:, :, ss_]), cr, ALU.mult)
                    k.tt("dve", tb, Eimf.v(e3i[:, :, ss_]), ci, ALU.mult)
                    k.tt("dve", Eref.v(e3r[:, :, dd_]), ta, tb, ALU.subtract)
                    k.tt("dve", ta, Eref.v(e3r[:, :, ss_]), ci, ALU.mult)
                    k.tt("dve", tb, Eimf.v(e3i[:, :, ss_]), cr, ALU.mult)
                    k.tt("dve", Eimf.v(e3i[:, :, dd_]), ta, tb, ALU.add)
                n *= 2
            k.copy("act", Ere[:, :], Eref[:, :]); k.copy("act", Eim[:, :], Eimf[:, :])
            crb = sm["cre"].v(sm["cre"].ap.unsqueeze(2).to_broadcast([128, NCH, 16]))
            cib = sm["cim"].v(sm["cim"].ap.unsqueeze(2).to_broadcast([128, NCH, 16]))
            b3 = [Bch[ri].v(r3(Bch[ri].ap, 16)) for ri in range(2)]
            tA3 = tA.v(r3(tA.ap, 16)); tB3 = tB.v(r3(tB.ap, 16))
            k.tt("dve", tA3, b3[0], crb, ALU.mult); k.tt("dve", tB3, b3[1], cib, ALU.mult)
            k.tt("dve", Bh[0].v(r3(Bh[0].ap, 16)), tA3, tB3, ALU.subtract)
            k.tt("dve", tA3, b3[1], crb, ALU.mult); k.tt("dve", tB3, b3[0], cib, ALU.mult)
            k.tt("dve", Bh[1].v(r3(Bh[1].ap, 16)), tA3, tB3, ALU.add)
            for ri in range(2):
                k.memset("dve", Bpf[:, :], 0.0)
                for gl in range(2):
                    dst = Bpf.ap[gl * 64:(gl + 1) * 64, :].rearrange("p (a b x) -> p a b x", b=4, x=128)
                    for bb in range(4):
                        k.copy("dve", Bpf.v(dst[:, :, bb, bb * 32 + gl * 16: bb * 32 + gl * 16 + 16]),
                               Bh[ri].v(Bh[ri].ap[gl * 64:(gl + 1) * 64, :].rearrange("p (a b c) -> p a b c", b=4, c=16)[:, :, bb, :]))
                for cc in range(NCH):
                    ps = PS[cc % 2]
                    k.tr(ps[:, 0:128], Bpf[:, cc * 128:(cc + 1) * 128], ident_f[:, :])
                    k.copy("act", Bp[ri][:, cc * 128:(cc + 1) * 128], ps[:, 0:128])
            for ri, nm in enumerate(("s5_c_re_" + sfx, "s5_c_im_" + sfx)):
                src = s5in[nm]
                k.dma(Cn.v(r3(Cn.ap, 64)), src.v(src.ap.rearrange("(fc g8) co p -> (g8 co) fc p", g8=8)))
                c4 = Cpf.ap.rearrange("p (a b q) -> p a b q", b=4, q=128)
                for bb in range(4):
                    for gl in range(2):
                        k.ts("dve", Cpf.v(c4[:, :, bb, gl * 64:(gl + 1) * 64]), Cn.v(r3(Cn.ap, 64)),
                             rmask[:, bb * 2 + gl:bb * 2 + gl + 1], (1.0 if ri == 0 else -1.0), ALU.mult, ALU.mult)
                for cc in range(NCH):
                    ps = PS[2 + cc % 2]
                    k.tr(ps[:, 0:128], Cpf[:, cc * 128:(cc + 1) * 128], ident_f[:, :])
                    k.copy("act", Cp[ri][:, cc * 128:(cc + 1) * 128], ps[:, 0:128])
            for ri in range(2):
                k.memset("dve", carry[ri][:, :], 0.0)
            k.barrier()
            A.off = work0
            NB = 2
            uT = [A.bf16("uT", 1024) for _ in range(NB)]
            rhoT = A.f32("rhoT", 4096)
            r3t = r3(rhoT.ap, 128)
            k.copy("dve", rhoT.v(r3t), rho.v(rho.ap.unsqueeze(2).to_broadcast([128, NCH, 128])))
            first = 0 if d_ == 0 else 127
            last = 127 if d_ == 0 else 0
            k.memset("dve", rhoT.v(r3t[:, :, first:first + 1]), 0.0)
            W = []
            for _ in range(2):
                W.append({n: A.bf16(n, 1024) for n in ("bur", "bui", "t1", "t2", "t3", "t4", "zri", "zii", "zr", "zi", "srb", "sib")})
            tmp8 = [A.f32("tmp8", 8) for _ in range(2)]
            yf = [A.f32("yf", 1024) for _ in range(NB)]
            if d_ == 1:
                zf = A.f32("zf", 1024); zb = A.bf16("zb", 1024); q1 = A.f32("q1", 1024); q2 = A.f32("q2", 1024)
            cnt = [0]

            def mix_group(i, b, g2):
                w = W[cnt[0] % 2]
                cnt[0] += 1
                u3 = r3(uT[b].ap, 128)
                for j8 in range(8):
                    cc = g2 * 8 + j8
                    fc = cc // 4
                    k.mm(PS[0 + j8 // 4][:, (j8 % 4) * 128:(j8 % 4 + 1) * 128], Bp[0][:, cc * 128:(cc + 1) * 128], uT[b].v(u3[:, fc, :]))
                    k.mm(PS[2 + j8 // 4][:, (j8 % 4) * 128:(j8 % 4 + 1) * 128], Bp[1][:, cc * 128:(cc + 1) * 128], uT[b].v(u3[:, fc, :]))
                for hb in range(2):
                    k.copy("act", w["bur"][:, hb * 512:(hb + 1) * 512], PS[0 + hb][:, :])
                    k.copy("act", w["bui"][:, hb * 512:(hb + 1) * 512], PS[2 + hb][:, :])
                er = Ere[:, g2 * 1024:(g2 + 1) * 1024]; ei = Eim[:, g2 * 1024:(g2 + 1) * 1024]
                k.tt("dve", w["t1"][:, :], w["bur"][:, :], er, ALU.mult)
                k.tt("dve", w["t2"][:, :], w["bui"][:, :], ei, ALU.mult)
                k.tt("dve", w["t3"][:, :], w["bui"][:, :], er, ALU.mult)
                k.tt("dve", w["t4"][:, :], w["bur"][:, :], ei, ALU.mult)
                k.tt("dve", w["zri"][:, :], w["t1"][:, :], w["t2"][:, :], ALU.subtract)
                k.tt("dve", w["zii"][:, :], w["t3"][:, :], w["t4"][:, :], ALU.add)
                cs = slice(g2 * 8, (g2 + 1) * 8)
                for ci, (zin, zo) in enumerate(((w["zri"], w["zr"]), (w["zii"], w["zi"]))):
                    k.tt("dve", tmp8[ci][:, :], carry[ci][:, cs], rho[:, cs], ALU.mult)
                    zf_ = zin.v(r3(zin.ap, 128)[:, :, first])
                    k.tt("dve", zf_, zf_, tmp8[ci][:, :], ALU.add)
                    rt = rhoT[:, g2 * 1024:(g2 + 1) * 1024]
                    if d_ == 0:
                        k.scan(zo[:, :], rt, zin[:, :], 0.0)
                    else:
                        k.scan(zo[:, ::-1], rhoT.v(rhoT.ap[:, g2 * 1024:(g2 + 1) * 1024][:, ::-1]), zin[:, ::-1], 0.0)
                k.tt("dve", w["t1"][:, :], w["zr"][:, :], er, ALU.mult)
                k.tt("dve", w["t2"][:, :], w["zi"][:, :], ei, ALU.mult)
                k.tt("dve", w["t3"][:, :], w["zi"][:, :], er, ALU.mult)
                k.tt("dve", w["t4"][:, :], w["zr"][:, :], ei, ALU.mult)
                k.tt("dve", w["srb"][:, :], w["t1"][:, :], w["t2"][:, :], ALU.add)
                k.tt("dve", w["sib"][:, :], w["t3"][:, :], w["t4"][:, :], ALU.subtract)
                k.copy("act", carry[0][:, cs], w["srb"].v(r3(w["srb"].ap, 128)[:, :, last]))
                k.copy("act", carry[1][:, cs], w["sib"].v(r3(w["sib"].ap, 128)[:, :, last]))
                return w

            def mix_dir(i, b, fc, py):
                if fc % 2 == 0:
                    mix_dir.w = mix_group(i, b, fc // 2)
                w = mix_dir.w
                for j4 in range(4):
                    cc = fc * 4 + j4
                    o = (fc % 2) * 512 + j4 * 128
                    k.mm(py[:, 0:128], Cp[0][:, cc * 128:(cc + 1) * 128], w["srb"][:, o:o + 128], start=(j4 == 0), stop=False)
                    k.mm(py[:, 0:128], Cp[1][:, cc * 128:(cc + 1) * 128], w["sib"][:, o:o + 128], start=False, stop=(j4 == 3 and d_ == 0))

            if d_ == 0:
                for i in range(NT):
                    b = i % NB
                    k.dma(uT[b].v(r3(uT[b].ap, 128)), hnT_d[:, :, i * 128:(i + 1) * 128])
                    for fc in range(8):
                        py = PS[4 + fc % 2]
                        mix_dir(i, b, fc, py)
                        k.copy("act", yf[b][:, fc * 128:(fc + 1) * 128], py[:, 0:128])
                    k.dma(yS_d[:, :, i * 128:(i + 1) * 128], yf[b].v(r3(yf[b].ap, 128)))
            else:
                for i in range(NT - 1, -1, -1):
                    b = i % NB
                    u3 = r3(uT[b].ap, 128)
                    k.dma(uT[b].v(u3), hnT_d[:, :, i * 128:(i + 1) * 128])
                    k.dma(yf[b].v(r3(yf[b].ap, 128)), yS_d[:, :, i * 128:(i + 1) * 128])
                    for fc in range(8):
                        py = PS[4 + fc % 2]
                        mix_dir(i, b, fc, py)
                        k.mm(py[:, 0:128], Dd[:, fc * 128:(fc + 1) * 128], uT[b].v(u3[:, fc, :]), start=False, stop=True)
                        k.tt("dve", zf[:, fc * 128:(fc + 1) * 128], py[:, 0:128], yf[b][:, fc * 128:(fc + 1) * 128], ALU.add)
                    k.act(q1[:, :], zf[:, :], AF.Square)
                    k.ts("dve", q1[:, :], q1[:, :], 0.044715, 1.0, ALU.mult, ALU.add)
                    k.tt("dve", q2[:, :], q1[:, :], zf[:, :], ALU.mult)
                    k.act(q1[:, :], q2[:, :], AF.Sigmoid, scale=1.5957691216057308)
                    k.tt("dve", zf[:, :], zf[:, :], q1[:, :], ALU.mult)
                    k.copy("act", zb[:, :], zf[:, :])
                    for fo in range(8):
                        pg = PS[6 + fo % 2]
                        for c in range(8):
                            k.mm(pg[:, 0:128], gw[:, c * D + fo * 128:c * D + (fo + 1) * 128], zb[:, c * 128:(c + 1) * 128], start=(c == 0), stop=(c == 7))
                        k.act(q2[:, fo * 128:(fo + 1) * 128], pg[:, 0:128], AF.Sigmoid, bias=gb[:, fo:fo + 1])
                    k.tt("dve", q2[:, :], zf[:, :], q2[:, :], ALU.mult)
                    pt = PS[6]
                    for half in range(2):
                        for c in range(4):
                            cc = half * 4 + c
                            k.tr(pt[:, c * 128:(c + 1) * 128], q2[:, cc * 128:(cc + 1) * 128], ident_f[:, :])
                        k.copy("act", q1[:, half * 512:(half + 1) * 512], pt[:, :])
                    k.dma(m_d[i * 128:(i + 1) * 128, :], q1[:, :])
            k.barrier()

    if stage == 2:
        phase_norm(x_in, 1)
        phase_s5()


    rin = {}
    for nm, shp in (("rwkv_w0_f", [1, 512]), ("rwkv_w_up_f", [64, 512]), ("rwkv_w0_b", [1, 512]), ("rwkv_w_up_b", [64, 512]),
                    ("rwkv_a0", [1, 512]), ("rwkv_a_up", [64, 512]), ("rwkv_g_up", [128, 512]), ("rwkv_k_k", [1, 512]),
                    ("rwkv_k_a", [1, 512]), ("rwkv_r_k", [1, 512]), ("rwkv_lnx_g", [1, 512]), ("rwkv_lnx_b", [1, 512]),
                    ("trimask", [128, 512]), ("chunkmask", [128, 2])):
        rin[nm] = din(nm, shp)
    yF_d = dscr("yF", [T, 512])
    yaT_d = Buf(k, "yaT", nc.dram_tensor("yaT", [128, 4, T], BF16, kind="ExternalOutput" if stage == 3 else "Internal").ap())
    LOGW_SCALE = -math.exp(-0.5)

    def phase_rwkv():
        A.off = const_end
        tri = A.f32("tri", 512)
        k.dma(tri[:, :], rin["trimask"][:, :])
        cmask = A.f32("cmask", 2)
        k.dma(cmask[:, :], rin["chunkmask"][:, :])
        LE, LT, GE, GT = (tri[:, j * 128:(j + 1) * 128] for j in range(4))
        gmask = [A.f32("gmask", 512) for _ in range(2)]
        nmask = [A.f32("nmask", 512) for _ in range(2)]
        for d_, (m1, m2, m3) in enumerate(((1, 0, 3), (3, 2, 1))):
            for j, mi in enumerate((m1, m2, m3, m3)):
                k.copy("dve", gmask[d_][:, j * 128:(j + 1) * 128], tri[:, mi * 128:(mi + 1) * 128])
            for j in range(4):
                k.copy("dve", nmask[d_][:, j * 128:(j + 1) * 128], tri[:, m2 * 128:(m2 + 1) * 128])
        identb4 = A.bf16("identb4", 512)
        for j in range(4):
            k.copy("dve", identb4[:, j * 128:(j + 1) * 128], ident_f[:, :])
        bc = {}
        for nm in ("rwkv_w0_f", "rwkv_w0_b", "rwkv_a0", "rwkv_k_k", "rwkv_k_a", "rwkv_r_k", "rwkv_lnx_g", "rwkv_lnx_b"):
            bc[nm] = A.f32(nm, 512)
            k.dma(bc[nm][:, :], rin[nm].v(rin[nm].ap[0:1, :].to_broadcast([128, 512])))
        omka = A.f32("omka", 512)
        k.ts("dve", omka[:, :], bc["rwkv_k_a"][:, :], -1.0, 1.0, ALU.mult, ALU.add)
        stg = A.f32("stgr", 512)
        wup = A.bf16("wup", 512); aup = A.bf16("aup", 512); gup = A.bf16("gup", 512)
        k.dma(stg[0:64, :], rin["rwkv_w_up_f"][:, :]); k.dma(stg[64:128, :], rin["rwkv_w_up_b"][:, :])
        k.copy("dve", wup[:, :], stg[:, :])
        k.dma(stg[0:64, :], rin["rwkv_a_up"][:, :])
        k.copy("dve", aup[0:64, :], stg[0:64, :])
        k.dma(stg[:, :], rin["rwkv_g_up"][:, :])
        k.copy("dve", gup[:, :], stg[:, :])
        hsb = [A.f32("hsb", RW) for _ in range(2)]
        Lb = A.bf16("Lb", 384); LTb = A.bf16("LTb", 384)
        sig = A.f32("sig", 512); asg = A.f32("asg", 512); gg = A.f32("gg", 512)
        kk = A.f32("kk", 512); sq = A.f32("sq", 512); st8 = A.f32("st8", 32)
        kmod = A.f32("kmod", 512); bb = A.f32("bb", 512); tmp = A.f32("tmp", 512); bonus = A.f32("bonus", 512)
        logw = A.f32("logw", 512)
        eP = A.f32("eP", 512); eM = A.f32("eM", 512); eX = A.f32("eX", 512); eR = A.f32("eR", 512)
        WC = A.f32("WC", 16)
        tm = {n: A.bf16(n, 512) for n in ("rt", "kt", "bt", "at", "kp", "bp", "v", "vc0", "vc1")}
        fm = {n: A.bf16("fm" + n, 1024) for n in ("at", "rt", "bt", "kt")}
        G = [A.bf16("G", 512) for _ in range(8)]
        Nkr = A.bf16("Nkr", 1024)
        X = [A.bf16("X", 1024) for _ in range(2)]; XT = [A.bf16("XT", 1024) for _ in range(2)]
        Am = [A.bf16("Am", 1024) for _ in range(2)]
        Atp = A.bf16("Atp", 1024)
        Pm = A.bf16("Pm", 1024)
        Uc = A.bf16("Uc", 512)
        Sf = A.f32("Sf", 512); Sb = A.bf16("Sb", 512); Stmp = A.f32("Stmp", 512)
        yacc = A.f32("yacc", 512)
        yfl = A.f32("yfl", 512)
        yab = A.bf16("yab", 512); yaT = A.bf16("yaT", 512)
        assert fm["rt"].ap.offset == fm["at"].ap.offset + 1024 or True

        def prep(i, d_):
            b = i % 2
            h_ = hsb[b]
            k.dma(h_[:, :], hs_d[i * 128:(i + 1) * 128, 0:RW])
            r_ = h_[:, 0:512]; k_ = h_[:, 512:1024]; v_ = h_[:, 1024:1536]
            k.act(Lb[:, 0:128], h_[:, 1536:1664], AF.Tanh)
            k.copy("dve", Lb[:, 128:192], h_[:, 1664:1728])
            k.act(Lb[:, 256:384], h_[:, 1728:1856], AF.Sigmoid)
            pt = PS[0]; ptb = pt.ap.bitcast(BF16)
            k.tr(pt.v(ptb[:, 0:128]), Lb[:, 0:128], ident_b[:, :])
            k.tr(pt.v(ptb[0:64, 128:256]), Lb[:, 128:192], ident_b[:, :])
            k.tr(pt.v(ptb[:, 256:384]), Lb[:, 256:384], ident_b[:, :])
            k.copy("dve", LTb[:, 0:128], pt.v(ptb[:, 0:128]))
            k.copy("dve", LTb[0:64, 128:256], pt.v(ptb[0:64, 128:256]))
            k.copy("dve", LTb[:, 256:384], pt.v(ptb[:, 256:384]))
            hp = d_ * 64
            k.mm(PS[1][:, :], LTb[hp:hp + 64, 0:128], wup[hp:hp + 64, :])
            k.mm(PS[2][:, :], LTb[0:64, 128:256], aup[0:64, :])
            k.mm(PS[3][:, :], LTb[:, 256:384], gup[:, :])
            k.tt("dve", tmp[:, :], PS[1][:, :], bc["rwkv_w0_b" if d_ else "rwkv_w0_f"][:, :], ALU.add)
            k.act(sig[:, :], tmp[:, :], AF.Sigmoid)
            k.ts("dve", logw[:, :], sig[:, :], LOGW_SCALE, None, ALU.mult)
            k.tt("dve", tmp[:, :], PS[2][:, :], bc["rwkv_a0"][:, :], ALU.add)
            k.act(asg[:, :], tmp[:, :], AF.Sigmoid)
            k.copy("dve", gg[:, :], PS[3][:, :])
            k.tt("dve", kk[:, :], k_, bc["rwkv_k_k"][:, :], ALU.mult)
            k.act(sq[:, :], kk[:, :], AF.Square)
            k.reduce(st8[:, 0:8], sq.v(r3(sq.ap, 64)))
            k.ts("dve", st8[:, 8:16], st8[:, 0:8], 1e-24, None, ALU.add)
            k.act(st8[:, 16:24], st8[:, 8:16], AF.Sqrt)
            k.recip(st8[:, 24:32], st8[:, 16:24])
            k.tt("dve", kk.v(r3(kk.ap, 64)), kk.v(r3(kk.ap, 64)),
                 st8.v(st8.ap[:, 24:32].unsqueeze(2).to_broadcast([128, 8, 64])), ALU.mult)
            k.tt("dve", tmp[:, :], asg[:, :], bc["rwkv_k_a"][:, :], ALU.mult)
            k.tt("dve", tmp[:, :], tmp[:, :], omka[:, :], ALU.add)
            k.tt("dve", kmod[:, :], k_, tmp[:, :], ALU.mult)
            k.tt("dve", bb[:, :], kk[:, :], asg[:, :], ALU.mult)
            mi, me, mr = ((LE, LT, GT) if d_ == 0 else (GE, GT, LT))
            k.mm(PS[4][:, :], mi, logw[:, :])
            k.mm(PS[5][:, :], me, logw[:, :])
            k.mm(PS[6][:, :], mr, logw[:, :])
            k.act(eP[:, :], PS[4][:, :], AF.Exp)
            k.act(eM[:, :], PS[4][:, :], AF.Exp, scale=-1.0)
            k.act(eX[:, :], PS[5][:, :], AF.Exp)
            k.act(eR[:, :], PS[6][:, :], AF.Exp)
            pw = PS[7]
            for hd in range(8):
                k.mm(pw[0:64, hd * 2:(hd + 1) * 2], logw[:, hd * 64:(hd + 1) * 64], cmask[:, :])
            k.act(WC[0:64, :], pw[0:64, 0:16], AF.Exp)
            k.tt("dve", tm["rt"][:, :], r_, eP[:, :], ALU.mult)
            k.tt("dve", tm["kt"][:, :], kmod[:, :], eM[:, :], ALU.mult)
            k.tt("dve", tm["bt"][:, :], bb[:, :], eM[:, :], ALU.mult)
            k.stt("dve", tm["at"][:, :], kk[:, :], -1.0, eX[:, :], ALU.mult, ALU.mult)
            k.tt("dve", tm["kp"][:, :], kmod[:, :], eR[:, :], ALU.mult)
            k.tt("dve", tm["bp"][:, :], bb[:, :], eR[:, :], ALU.mult)
            k.copy("act", tm["v"][:, :], v_)
            k.act(tm["vc0"][:, :], v_, AF.Copy, scale=cmask[:, 0:1])
            k.act(tm["vc1"][:, :], v_, AF.Copy, scale=cmask[:, 1:2])
            if d_ == 1:
                k.tt("dve", tmp[:, :], r_, kmod[:, :], ALU.mult)
                k.tt("dve", tmp[:, :], tmp[:, :], bc["rwkv_r_k"][:, :], ALU.mult)
                k.reduce(st8[:, 0:8], tmp.v(r3(tmp.ap, 64)))
                k.tt("dve", bonus.v(r3(bonus.ap, 64)), h_.v(r3(h_.ap[:, 1024:1536], 64)),
                     st8.v(st8.ap[:, 0:8].unsqueeze(2).to_broadcast([128, 8, 64])), ALU.mult)
            for ti, n in enumerate(("at", "rt", "bt", "kt")):
                pt_ = PS[ti % 2]; pb_ = pt_.ap.bitcast(BF16)
                for hd in range(8):
                    k.tr(pt_.v(pb_[0:64, hd * 128:(hd + 1) * 128]), tm[n][:, hd * 64:(hd + 1) * 64], ident_b[:, :])
                k.copy("dve" if ti % 2 else "act", fm[n][0:64, :], pt_.v(pb_[0:64, 0:1024]))

        def chunk_alg(i, d_):
            for hd in range(8):
                hs_ = slice(hd * 128, (hd + 1) * 128)
                pg = PS[2 + hd % 2]
                k.mm(pg[:, 0:128], fm["bt"][0:64, hs_], fm["at"][0:64, hs_])
                k.mm(pg[:, 128:256], fm["bt"][0:64, hs_], fm["rt"][0:64, hs_])
                k.mm(pg[:, 256:384], fm["at"][0:64, hs_], fm["bt"][0:64, hs_])
                k.mm(pg[:, 384:512], fm["at"][0:64, hs_], fm["kt"][0:64, hs_])
                k.tt("dve", G[hd][:, :], pg[:, :], gmask[d_][:, :], ALU.mult)
                pn = PS[4]
                k.mm(pn[:, (hd % 4) * 128:(hd % 4 + 1) * 128], fm["kt"][0:64, hs_], fm["rt"][0:64, hs_])
                if hd % 4 == 3:
                    k.tt("dve", Nkr[:, (hd // 4) * 512:(hd // 4 + 1) * 512], pn[:, :], nmask[d_][:, :], ALU.mult)
            for g4 in range(2):
                gs = slice(g4 * 512, (g4 + 1) * 512)
                for hh in range(4):
                    hd = g4 * 4 + hh
                    k.copy("act", X[0][:, hd * 128:(hd + 1) * 128], G[hd][:, 0:128])
                    k.copy("act", XT[0][:, hd * 128:(hd + 1) * 128], G[hd][:, 256:384])
                k.tt("dve", Am[0][:, gs], X[0][:, gs], identb4[:, :], ALU.add)
            cur = 0
            for lev in range(5):
                nxt = 1 - cur
                for g4 in range(2):
                    gs = slice(g4 * 512, (g4 + 1) * 512)
                    for hh in range(4):
                        hd = g4 * 4 + hh
                        hs_ = slice(hd * 128, (hd + 1) * 128); ps_ = slice(hh * 128, (hh + 1) * 128)
                        if lev < 4:
                            k.mm(PS[5][:, ps_], XT[cur][:, hs_], X[cur][:, hs_])
                        k.mm(PS[6][:, ps_], X[cur][:, hs_], XT[cur][:, hs_])
                    if lev < 4:
                        k.copy("act", X[nxt][:, gs], PS[5][:, :])
                    k.copy("dve", XT[nxt][:, gs], PS[6][:, :])
                    for hh in range(4):
                        hd = g4 * 4 + hh
                        hs_ = slice(hd * 128, (hd + 1) * 128); ps_ = slice(hh * 128, (hh + 1) * 128)
                        k.mm(PS[7][:, ps_], XT[nxt][:, hs_], Am[cur][:, hs_])
                    k.tt("dve", Am[nxt][:, gs], PS[7][:, :], Am[cur][:, gs], ALU.add)
                cur = nxt
            Tinv = Am[cur]
            for g4 in range(2):
                for hh in range(4):
                    hd = g4 * 4 + hh
                    hs_ = slice(hd * 128, (hd + 1) * 128); ps_ = slice(hh * 128, (hh + 1) * 128)
                    k.mm(PS[0][0:64, ps_], tm["at"][:, hd * 64:(hd + 1) * 64], Tinv[:, hs_])
                    k.mm(PS[1][:, ps_], G[hd][:, 384:512], Tinv[:, hs_])
                k.copy("act", Atp[0:64, g4 * 512:(g4 + 1) * 512], PS[0][0:64, :])
                k.copy("dve", Pm[:, g4 * 512:(g4 + 1) * 512], PS[1][:, :])
            k.memset("dve", yacc[:, :], 0.0)
            for c in ((0, 1) if d_ == 0 else (1, 0)):
                vc = tm["vc%d" % c]
                pu, py, pd = PS[5], PS[6], PS[7]
                for hd in range(8):
                    hs_ = slice(hd * 128, (hd + 1) * 128); vs_ = slice(hd * 64, (hd + 1) * 64)
                    k.mm(pu[:, vs_], Atp[0:64, hs_], Sb[0:64, vs_], start=True, stop=False)
                    k.mm(pu[:, vs_], Pm[:, hs_], tm["v"][:, vs_], start=False, stop=True)
                k.ts("dve", Uc[:, :], pu[:, :], cmask[:, c:c + 1], None, ALU.mult)
                for hd in range(8):
                    hs_ = slice(hd * 128, (hd + 1) * 128); vs_ = slice(hd * 64, (hd + 1) * 64)
                    k.mm(py[:, vs_], fm["rt"][0:64, hs_], Sb[0:64, vs_], start=True, stop=False)
                    k.mm(py[:, vs_], G[hd][:, 128:256], Uc[:, vs_], start=False, stop=False)
                    k.mm(py[:, vs_], Nkr[:, hs_], tm["v"][:, vs_], start=False, stop=True)
                    k.mm(pd[0:64, vs_], tm["bp"][:, vs_], Uc[:, vs_], start=True, stop=False)
                    k.mm(pd[0:64, vs_], tm["kp"][:, vs_], vc[:, vs_], start=False, stop=True)
                k.stt("dve", yacc[:, :], py[:, :], cmask[:, c:c + 1], yacc[:, :], ALU.mult, ALU.add)
                wcb = WC.v(WC.ap[0:64, :].rearrange("p (h c) -> p h c", c=2)[:, :, c:c + 1].to_broadcast([64, 8, 64]))
                k.tt("dve", Stmp.v(r3(Stmp.ap[0:64, :], 64)), Sf.v(r3(Sf.ap[0:64, :], 64)), wcb, ALU.mult)
                k.tt("dve", Sf[0:64, :], Stmp[0:64, :], pd[0:64, :], ALU.add)
                k.copy("act", Sb[0:64, :], Sf[0:64, :])

        for d_ in range(2):
            k.memset("dve", Sf[:, :], 0.0)
            k.memset("dve", Sb[:, :], 0.0)
            order = range(NT) if d_ == 0 else range(NT - 1, -1, -1)
            for i in order:
                prep(i, d_)
                chunk_alg(i, d_)
                if d_ == 0:
                    k.dma(yF_d[i * 128:(i + 1) * 128, :], yacc[:, :])
                else:
                    k.dma(yfl[:, :], yF_d[i * 128:(i + 1) * 128, :])
                    k.tt("dve", yfl[:, :], yfl[:, :], yacc[:, :], ALU.add)
                    y3 = yfl.v(r3(yfl.ap, 64))
                    k.reduce(st8[:, 0:8], y3)
                    k.ts("dve", st8[:, 8:16], st8[:, 0:8], 1.0 / 64, None, ALU.mult)
                    k.tt("dve", y3, y3, st8.v(st8.ap[:, 8:16].unsqueeze(2).to_broadcast([128, 8, 64])), ALU.subtract)
                    k.act(sq[:, :], yfl[:, :], AF.Square)
                    k.reduce(st8[:, 0:8], sq.v(r3(sq.ap, 64)))
                    k.ts("dve", st8[:, 8:16], st8[:, 0:8], 1.0 / 64, 64e-5, ALU.mult, ALU.add)
                    k.act(st8[:, 16:24], st8[:, 8:16], AF.Sqrt)
                    k.recip(st8[:, 24:32], st8[:, 16:24])
                    k.tt("dve", y3, y3, st8.v(st8.ap[:, 24:32].unsqueeze(2).to_broadcast([128, 8, 64])), ALU.mult)
                    k.tt("dve", yfl[:, :], yfl[:, :], bc["rwkv_lnx_g"][:, :], ALU.mult)
                    k.tt("dve", yfl[:, :], yfl[:, :], bc["rwkv_lnx_b"][:, :], ALU.add)
                    k.tt("dve", yfl[:, :], yfl[:, :], bonus[:, :], ALU.add)
                    k.tt("dve", yab[:, :], yfl[:, :], gg[:, :], ALU.mult)
                    pt_ = PS[0]; pb_ = pt_.ap.bitcast(BF16)
                    for c in range(4):
                        k.tr(pt_.v(pb_[:, c * 128:(c + 1) * 128]), yab[:, c * 128:(c + 1) * 128], ident_b[:, :])
                    k.copy("act", yaT[:, :], pt_.v(pb_[:, 0:512]))
                    k.dma(yaT_d[:, :, i * 128:(i + 1) * 128], yaT.v(r3(yaT.ap, 128)))
        k.barrier()

    if stage == 3:
        phase_norm(x_in, 0)
        phase_inproj()
        phase_rwkv()

    def outproj_alloc():
        wo = A.bf16("wo_b", 8 * D)
        woa = A.bf16("wo_a", 4 * D)
        stg = A.f32("stgo", D // 2)
        for h_ in range(16):
            hh, hf = h_ // 2, h_ % 2
            k.dma(stg[0:64, :], w_out[512 + hh * 64:512 + (hh + 1) * 64, hf * 512:(hf + 1) * 512])
            k.copy("dve", wo[0:64, hh * D + hf * 512:hh * D + (hf + 1) * 512], stg[0:64, :])
        for h_ in range(8):
            c, hf = h_ // 2, h_ % 2
            k.dma(stg[:, :], w_out[c * 128:(c + 1) * 128, hf * 512:(hf + 1) * 512])
            k.copy("dve", woa[:, c * D + hf * 512:c * D + (hf + 1) * 512], stg[:, :])
        ybt = A.bf16("ybt", 1024)
        yat = A.bf16("yat", 512)
        return wo, woa, ybt, yat

    def outproj_pre(state, i, xtb):
        wo, woa, yb, ya = state
        y3 = r3(yb.ap, 128)
        a3 = r3(ya.ap, 128)
        k.dma(yb.v(y3[0:64, :, :]), ybT_d[:, :, i * 128:(i + 1) * 128])
        k.dma(ya.v(a3), yaT_d[:, :, i * 128:(i + 1) * 128])
        for j in range(2):
            ps = PS[6 + j]
            for c in range(4):
                k.mm(ps[:, :], ya.v(a3[:, c, :]), woa[:, c * D + j * 512:c * D + (j + 1) * 512], start=(c == 0), stop=False)
            for h_ in range(8):
                k.mm(ps[:, :], yb.v(y3[0:64, h_, :]), wo[0:64, h_ * D + j * 512:h_ * D + (j + 1) * 512], start=False, stop=(h_ == 7))
            k.tt("dve", xtb[:, j * 512:(j + 1) * 512], ps[:, :], xtb[:, j * 512:(j + 1) * 512], ALU.add)

    def s5add_alloc():
        return A.f32("mbuf", D)

    def s5add_pre(mb, i, xtb):
        k.dma(mb[:, :], m_d[i * 128:(i + 1) * 128, :])
        k.tt("dve", xtb[:, :], xtb[:, :], mb[:, :], ALU.add)

    if stage == 99:
        phase_norm(x_in, 0)
        phase_inproj()
        phase_attn()
        phase_rwkv()
        ffn_phase(0, x_in, x1_d, outproj_alloc, outproj_pre)
        phase_norm(x1_d, 1)
        phase_s5()
        ffn_phase(1, x1_d, y_out, s5add_alloc, s5add_pre)

    if stage == 0:
        ffn_phase(0, x_in, y_out)

    k.barrier()
    with nc.Block() as block:
        k.replay(block)
    return nc, st


def _rope_tab(T):
    t = np.arange(T)
    inv = (10000.0 ** (-np.arange(16, dtype=np.float32) / 16)).astype(np.float32)
    ang = np.concatenate([(t // 64)[:, None].astype(np.float32) * inv,
                          (t % 64)[:, None].astype(np.float32) * inv], -1)
    return np.concatenate([np.cos(ang), np.sin(ang)], -1).astype(np.float32)


def common_inputs(inputs, T):
    f = lambda n: np.ascontiguousarray(np.asarray(inputs[n], dtype=np.float32))
    common = dict(mix_norm=f("mix_norm"), ffn_norm=f("ffn_norm"), ffn_up=f("ffn_up"), ffn_down=f("ffn_down"),
                  hyb_w_in=f("hyb_w_in")[0], hyb_shift_mu=f("hyb_shift_mu"), hyb_w_out=f("hyb_w_out")[0],
                  identb=np.eye(128, dtype=np.float32), rope=_rope_tab(T),
                  att_q_norm=f("att_q_norm"), att_k_norm=f("att_k_norm"),
                  rowmask=(np.arange(128)[:, None] // 16 == np.arange(8)[None, :]).astype(np.float32))
    for n in ("s5_lam_re_f", "s5_lam_im_f", "s5_log_dt_f", "s5_lam_re_b", "s5_lam_im_b", "s5_log_dt_b",
              "s5_b_re", "s5_b_im", "s5_c_re_f", "s5_c_im_f", "s5_c_re_b", "s5_c_im_b", "s5_glu_w"):
        a = f(n)
        common[n] = a[0] if n != "s5_log_dt_f" and n != "s5_log_dt_b" else a
    for n in ("s5_d", "s5_glu_b"):
        common[n] = f(n)
    for n in ("rwkv_w0_f", "rwkv_w0_b", "rwkv_a0", "rwkv_k_k", "rwkv_k_a", "rwkv_lnx_g", "rwkv_lnx_b"):
        common[n] = f(n)
    common["rwkv_r_k"] = f("rwkv_r_k").reshape(1, 512)
    for n in ("rwkv_w_up_f", "rwkv_w_up_b", "rwkv_a_up", "rwkv_g_up"):
        common[n] = f(n)[0]
    t = np.arange(128)
    same = (t[:, None] // 64) == (t[None, :] // 64)
    tri = [same & (t[:, None] <= t[None, :]), same & (t[:, None] < t[None, :]),
           same & (t[:, None] >= t[None, :]), same & (t[:, None] > t[None, :])]
    common["trimask"] = np.concatenate(tri, 1).astype(np.float32)
    common["chunkmask"] = (t[:, None] // 64 == np.arange(2)[None, :]).astype(np.float32)
    return common


def kernel(**inputs):
    xp = np.asarray(inputs["x_prompt"]); xs = np.asarray(inputs["x_sample"])
    T = xp.shape[1]
    seqs = [xp[i] for i in range(xp.shape[0])] + [xs[i] for i in range(xs.shape[0])]
    nseq = len(seqs)
    nc, st = build(T, stage=99)
    common = common_inputs(inputs, T)
    in_maps = []
    for c in range(NCORES):
        m = dict(common)
        m["x"] = np.ascontiguousarray(seqs[c % nseq], dtype=np.float32)
        in_maps.append(m)
    res = run_bass_kernel_spmd(nc, in_maps, core_ids=list(range(NCORES)))
    ys = [np.asarray(res.results[c]["y"], dtype=np.float32) for c in range(nseq)]
    yp = np.stack(ys[:xp.shape[0]], 0)
    ysm = np.stack(ys[xp.shape[0]:], 0)
    return (yp, ysm)
```

```python
import math
import numpy as np
import concourse.bass as bass
import concourse.mybir as mybir
from concourse.bass_utils import run_bass_kernel_spmd

F32 = mybir.dt.float32
BF16 = mybir.dt.bfloat16
AF = mybir.ActivationFunctionType
ALU = mybir.AluOpType
AX = mybir.AxisListType

D = 1024
DFF = 4096
NCORES = 8
SAME_ENGINE_SYNC = True


class Dep:
    __slots__ = ("w", "r")

    def __init__(self):
        self.w = None
        self.r = {}


class V:
    __slots__ = ("ap", "dep")

    def __init__(self, ap, dep):
        self.ap = ap
        self.dep = dep


class Buf:
    def __init__(self, k, name, ap):
        self.k = k
        self.name = name
        self.ap = ap
        self.deps = {}

    def d(self, key=None):
        if key not in self.deps:
            self.deps[key] = Dep()
        return self.deps[key]

    def __getitem__(self, idx):
        return V(self.ap[idx], self.d(None))

    def v(self, ap, key=None):
        return V(ap, self.d(key))

    def key(self, key):
        b = self

        class _K:
            def __getitem__(s, idx):
                return V(b.ap[idx], b.d(key))
        return _K()


class Eng:
    def __init__(self, name, sem):
        self.name = name
        self.sem = sem
        self.cnt = 0
        self.seen = {}
        self.ops = []
        self.dsems = []
        self.dma_n = 0


class K:
    def __init__(self, nc, stack):
        self.nc = nc
        self.stack = stack
        self.eng = {}
        for n in ("pe", "act", "dve", "pool", "sp"):
            sem = stack.enter_context(nc.semaphore("s_" + n))
            self.eng[n] = Eng(n, sem)
        for n in ("sp", "pool", "act"):
            for i in range(8):
                self.eng[n].dsems.append(stack.enter_context(nc.semaphore("d_%s%d" % (n, i))))
        self.allsems = []
        self.nins = 0

    def _wait(self, e, ev):
        if ev is None:
            return
        src, sem, val = ev
        if src is e and not (SAME_ENGINE_SYNC and e.name in ("act", "dve", "pool")):
            return
        key = id(sem)
        if e.seen.get(key, 0) >= val:
            return
        e.seen[key] = val
        e.ops.append(("w", sem, val))

    def _deps(self, e, reads, writes):
        for v in reads:
            self._wait(e, v.dep.w)
        for v in writes:
            self._wait(e, v.dep.w)
            for o, ev in v.dep.r.items():
                if o != e.name or e.name != "pe":
                    self._wait(e, ev)

    def _mark(self, e, ev, reads, writes):
        for v in writes:
            v.dep.w = ev
            v.dep.r = {}
        for v in reads:
            v.dep.r[e.name] = ev

    def op(self, en, fn, reads, writes):
        e = self.eng[en]
        reads = [v for v in reads if isinstance(v, V)]
        self._deps(e, reads, writes)
        e.cnt += 1
        ev = (e, e.sem, e.cnt)
        e.ops.append(("i", fn, e.sem, 1))
        self._mark(e, ev, reads, writes)
        self.nins += 1

    def dma(self, out, in_, q="sp", slow=False):
        e = self.eng[q]
        n = e.dma_n
        e.dma_n += 1
        P = len(e.dsems)
        s = e.dsems[n % P]
        if n >= P:
            self._wait(e, (None, s, 16 * (n // P)))
        self._deps(e, [in_], [out])
        ev = (None, s, 16 * (n // P + 1))
        oa, ia = out.ap, in_.ap
        if slow:
            e.ops.append(("i", lambda h: h.dma_start(out=oa, in_=ia, allow_slow_non_contiguous=True), s, 16))
        else:
            e.ops.append(("i", lambda h: h.dma_start(out=oa, in_=ia), s, 16))
        out.dep.w = ev
        out.dep.r = {}
        in_.dep.r[("dma", q, n % P)] = ev
        self.nins += 1

    def barrier(self):
        evs = []
        for n, e in self.eng.items():
            if e.cnt:
                evs.append((e, e.sem, e.cnt))
            P = len(e.dsems)
            for i in range(min(P, e.dma_n)):
                cnt = (e.dma_n - 1 - i) // P + 1
                evs.append((None, e.dsems[i], 16 * cnt))
        for n, e in self.eng.items():
            for ev in evs:
                if ev[0] is e:
                    continue
                self._wait(e, ev)

    def replay(self, block):
        def mk(e):
            def f(h):
                for o in e.ops:
                    if o[0] == "w":
                        h.wait_ge(o[1], o[2])
                    else:
                        o[1](h).then_inc(o[2], o[3])
            return f
        block.tensor(mk(self.eng["pe"]))
        block.scalar(mk(self.eng["act"]))
        block.vector(mk(self.eng["dve"]))
        block.gpsimd(mk(self.eng["pool"]))
        block.sync(mk(self.eng["sp"]))

    def mm(self, out, lhsT, rhs, start=True, stop=True):
        o, l, r = out.ap, lhsT.ap, rhs.ap
        self.op("pe", lambda h: h.matmul(o, lhsT=l, rhs=r, start=start, stop=stop), [lhsT, rhs], [out])

    def tr(self, out, in_, ident):
        o, i, d = out.ap, in_.ap, ident.ap
        self.op("pe", lambda h: h.transpose(o, i, d), [in_, ident], [out])

    def act(self, out, in_, func, bias=None, scale=1.0, accum=None):
        o, i = out.ap, in_.ap
        kw = {}
        rd = [in_]
        wr = [out]
        if bias is not None:
            kw["bias"] = bias.ap if isinstance(bias, V) else bias
            rd.append(bias)
        if isinstance(scale, V):
            kw["scale"] = scale.ap
            rd.append(scale)
        else:
            kw["scale"] = scale
        if accum is not None:
            kw["accum_out"] = accum.ap
            wr.append(accum)
        self.op("act", lambda h: h.activation(out=o, in_=i, func=func, **kw), rd, wr)

    def tt(self, en, out, in0, in1, op):
        o, a, b = out.ap, in0.ap, in1.ap
        self.op(en, lambda h: h.tensor_tensor(out=o, in0=a, in1=b, op=op), [in0, in1], [out])

    def ts(self, en, out, in0, s1, s2, op0, op1=None, accum=None):
        o, a = out.ap, in0.ap
        x1 = s1.ap if isinstance(s1, V) else s1
        x2 = s2.ap if isinstance(s2, V) else s2
        kw = {}
        wr = [out]
        if op1 is not None:
            kw["op1"] = op1
        if accum is not None:
            kw["accum_out"] = accum.ap
            wr.append(accum)
        self.op(en, lambda h: h.tensor_scalar(out=o, in0=a, scalar1=x1, scalar2=x2, op0=op0, **kw),
                [in0, s1, s2], wr)

    def stt(self, en, out, in0, s, in1, op0, op1):
        o, a, b = out.ap, in0.ap, in1.ap
        x = s.ap if isinstance(s, V) else s
        self.op(en, lambda h: h.scalar_tensor_tensor(out=o, in0=a, scalar=x, in1=b, op0=op0, op1=op1),
                [in0, s, in1], [out])

    def copy(self, en, out, in_):
        o, i = out.ap, in_.ap
        if en == "act":
            self.op("act", lambda h: h.copy(out=o, in_=i), [in_], [out])
        else:
            self.op(en, lambda h: h.tensor_copy(out=o, in_=i), [in_], [out])

    def memset(self, en, out, val):
        o = out.ap
        self.op(en, lambda h: h.memset(o, val), [], [out])

    def scan(self, out, d0, d1, init, op0=ALU.mult, op1=ALU.add):
        o, a, b = out.ap, d0.ap, d1.ap
        x = init.ap if isinstance(init, V) else init
        self.op("dve", lambda h: h.tensor_tensor_scan(out=o, data0=a, data1=b, initial=x, op0=op0, op1=op1),
                [d0, d1, init], [out])

    def reduce(self, out, in_, op=ALU.add, axis=AX.X):
        o, i = out.ap, in_.ap
        self.op("dve", lambda h: h.tensor_reduce(out=o, in_=i, axis=axis, op=op), [in_], [out])

    def recip(self, out, in_):
        o, i = out.ap, in_.ap
        self.op("dve", lambda h: h.reciprocal(out=o, in_=i), [in_], [out])


RW = 1856
INW = 2624
EPS = 1e-6


class Arena:
    def __init__(self, k, ap_f32, ncols):
        self.k, self.ap, self.n, self.off, self.cnt = k, ap_f32, ncols, 0, 0

    def reset(self):
        self.off = 0

    def f32(self, name, cols):
        a = self.ap[:, self.off:self.off + cols]
        self.off += cols
        assert self.off <= self.n, (name, self.off)
        self.cnt += 1
        return Buf(self.k, "%s_%d" % (name, self.cnt), a)

    def bf16(self, name, cols):
        c32 = (cols + 1) // 2
        a = self.ap[:, self.off:self.off + c32].bitcast(BF16)[:, 0:cols]
        self.off += c32
        assert self.off <= self.n, (name, self.off)
        self.cnt += 1
        return Buf(self.k, "%s_%d" % (name, self.cnt), a)


def r3(ap, b):
    return ap.rearrange("p (a b) -> p a b", b=b)


def build(T, stage=99):
    from contextlib import ExitStack
    NT = T // 128
    nc = bass.Bass("TRN2", target_bir_lowering=False)
    st = ExitStack()
    k = K(nc, st)

    def din(name, shape, dt=F32):
        return Buf(k, name, nc.dram_tensor(name, list(shape), dt, kind="ExternalInput").ap())

    def dscr(name, shape, dt=F32):
        return Buf(k, name, nc.dram_tensor(name, list(shape), dt, kind="Internal").ap())

    x_in = din("x", [T, D])
    mix_norm = din("mix_norm", [2, D]); ffn_norm = din("ffn_norm", [2, D])
    ffn_up = din("ffn_up", [2, D, DFF]); ffn_down = din("ffn_down", [2, DFF, D])
    w_in = din("hyb_w_in", [D, INW]); mu = din("hyb_shift_mu", [1, RW])
    w_out = din("hyb_w_out", [D, D])
    identb_d = din("identb", [128, 128]);
    y_out = Buf(k, "y", nc.dram_tensor("y", [T, D], F32, kind="ExternalOutput").ap())
    hs_d = dscr("hs", [T, INW])
    x1_d = dscr("x1", [T, D])

    arena_t = st.enter_context(nc.sbuf_tensor("arena", [128, 51200], F32))
    A = Arena(k, arena_t, 51200)
    PS = [Buf(k, "ps%d" % i, st.enter_context(nc.psum_tensor("ps%d" % i, [128, 512], F32))) for i in range(8)]

    ident_f = A.f32("identf", 128)
    ident_b = A.bf16("identb", 128)
    k.dma(ident_f[:, :], identb_d[:, :])
    k.copy("dve", ident_b[:, :], ident_f[:, :])
    gT = A.f32("gT", 32)
    for i, src in enumerate((mix_norm, mix_norm, ffn_norm, ffn_norm)):
        row = src.ap[i % 2:i % 2 + 1, :].rearrange("o (c p) -> p (o c)", p=128)
        k.dma(gT.v(gT.ap[:, i * 8:(i + 1) * 8]), src.v(row), slow=True)
    const_end = A.off

    def norm_T(xt, gi, out3, tmp):
        sq, ss, xb, pst = tmp
        k.act(sq[:, :], xt, AF.Square, accum=ss[:, 0:1])
        k.ts("dve", ss[:, 1:2], ss[:, 0:1], 1.0 / D, EPS, ALU.mult, ALU.add)
        k.act(ss[:, 2:3], ss[:, 1:2], AF.Sqrt)
        k.recip(ss[:, 3:4], ss[:, 2:3])
        k.ts("dve", xb[:, :], xt, ss[:, 3:4], None, ALU.mult)
        pb = pst.ap.bitcast(BF16)
        for c in range(8):
            k.tr(pst.v(pb[:, c * 128:(c + 1) * 128]), xb[:, c * 128:(c + 1) * 128], ident_b[:, :])
        g3 = gT.ap[:, gi * 8:(gi + 1) * 8].rearrange("p (a o) -> p a o", o=1).to_broadcast([128, 8, 128])
        k.tt("dve", out3, pst.v(r3(pb[:, 0:1024], 128)), gT.v(g3), ALU.mult)

    def load_w_bf16(dst3, src_ap3, stage_buf, nchunk, cols, eng="pool"):
        for c in range(nchunk):
            k.dma(stage_buf[:, 0:cols], src_ap3(c))
            k.copy(eng, dst3(c), stage_buf[:, 0:cols])

    def ffn_phase(layer, src_d, dst_d, pre_alloc=None, pre=None):
        A.off = const_end
        pre_state = pre_alloc() if pre_alloc else None
        wu = A.bf16("wu", 8 * DFF); wd = A.bf16("wd", 32 * D)
        stg = [A.f32("stg0", 1024), A.f32("stg1", 1024)] if pre_alloc is not outproj_alloc else [A.f32("stg0", 1024)] * 2
        for c in range(32):
            s = stg[c % 2]
            k.dma(s[:, 0:1024], ffn_up.v(ffn_up.ap[layer, (c // 4) * 128:(c // 4 + 1) * 128, (c % 4) * 1024:(c % 4 + 1) * 1024]))
            k.copy("act" if c % 2 else "dve", wu[:, c * 1024:(c + 1) * 1024], s[:, 0:1024])
        for c in range(32):
            s = stg[c % 2]
            k.dma(s[:, 0:1024], ffn_down.v(ffn_down.ap[layer, c * 128:(c + 1) * 128, :]))
            k.copy("act" if c % 2 else "dve", wd[:, c * 1024:(c + 1) * 1024], s[:, 0:1024])
        NB = 2
        xt = [A.f32("xt", D) for _ in range(NB)]
        ss = [A.f32("ss", 4) for _ in range(NB)]
        xb = [A.bf16("xb", D) for _ in range(NB)]
        xT = [A.bf16("xT", D) for _ in range(NB)]
        h = [A.bf16("h", DFF)] * NB
        sq = A.bf16("sqj", D)
        hT = [A.bf16("hT", DFF)] * NB
        yo = xt
        def load_norm(i):
            b = i % NB
            k.dma(xt[b][:, :], src_d[i * 128:(i + 1) * 128, :])
            if pre:
                pre(pre_state, i, xt[b])
            norm_T(xt[b][:, :], 2 + layer, xT[b].v(r3(xT[b].ap, 128)), (sq, ss[b], xb[b], PS[0]))

        def up(b, q):
            for j in (2 * q, 2 * q + 1):
                ps = PS[1 + j % 3]
                for c in range(8):
                    k.mm(ps[:, :], xT[b][:, c * 128:(c + 1) * 128], wu[:, c * DFF + j * 512:c * DFF + (j + 1) * 512],
                         start=(c == 0), stop=(c == 7))
                hj = h[b].key(j)[:, j * 512:(j + 1) * 512]
                k.act(hj, ps[:, :], AF.Relu)
                if j % 2 == 0:
                    k.tt("dve", hj, hj, hj, ALU.mult)
                else:
                    k.act(hj, hj, AF.Square)

        def tr(b, q):
            ps = PS[4 + q % 2]
            pb = ps.ap.bitcast(BF16)
            for c in range(8):
                cc = q * 8 + c
                k.tr(ps.v(pb[:, c * 128:(c + 1) * 128]), h[b].key(cc // 4)[:, cc * 128:(cc + 1) * 128], ident_b[:, :])
            k.copy("dve" if q % 2 == 0 else "act", hT[b].key(q)[:, q * 1024:(q + 1) * 1024], ps.v(pb[:, 0:1024]))

        def down(b, q):
            for c in range(q * 8, q * 8 + 8):
                for j in range(2):
                    k.mm(PS[6 + j][:, :], hT[b].key(q)[:, c * 128:(c + 1) * 128], wd[:, c * D + j * 512:c * D + (j + 1) * 512],
                         start=(c == 0), stop=(c == 31))

        load_norm(0)
        up(0, 0)
        up(0, 1)
        for i in range(NT):
            b = i % NB
            nb = (i + 1) % NB
            if i + 1 < NT:
                load_norm(i + 1)
            tr(b, 0)
            up(b, 2)
            down(b, 0)
            tr(b, 1)
            up(b, 3)
            down(b, 1)
            tr(b, 2)
            if i + 1 < NT:
                up(nb, 0)
            down(b, 2)
            tr(b, 3)
            if i + 1 < NT:
                up(nb, 1)
            down(b, 3)
            for j in range(2):
                k.tt("dve", yo[b][:, j * 512:(j + 1) * 512], PS[6 + j][:, :], xt[b][:, j * 512:(j + 1) * 512], ALU.add)
            k.dma(dst_d[i * 128:(i + 1) * 128, :], yo[b][:, :])
        k.barrier()


    hnT_d = dscr("hnT", [128, 8, T], BF16)
    qT_d = dscr("qT", [64, 8, T], BF16)
    kT_d = dscr("kT", [64, 2, T], BF16)
    v_d = dscr("v_att", [T, 128], BF16)
    ybT_d = Buf(k, "ybT", nc.dram_tensor("ybT", [64, 8, T], BF16, kind="ExternalOutput" if stage in (1, 11, 12) else "Internal").ap())
    rope_d = din("rope", [T, 64])
    qn_d = din("att_q_norm", [1, 64]); kn_d = din("att_k_norm", [1, 64])

    def phase_norm(src_d, gi):
        A.off = const_end
        NB = 2
        xt = [A.f32("xt", D) for _ in range(NB)]
        sq = A.f32("sq", D); ss = [A.f32("ss", 4) for _ in range(NB)]
        xb = [A.bf16("xb", D) for _ in range(NB)]
        xT = [A.bf16("xT", D) for _ in range(NB)]
        for i in range(NT):
            b = i % NB
            k.dma(xt[b][:, :], src_d[i * 128:(i + 1) * 128, :])
            norm_T(xt[b][:, :], gi, xT[b].v(r3(xT[b].ap, 128)), (sq, ss[b], xb[b], PS[i % 2]))
            k.dma(hnT_d[:, :, i * 128:(i + 1) * 128], xT[b].v(r3(xT[b].ap, 128)))
        k.barrier()

    def phase_inproj():
        A.off = const_end
        W1 = A.bf16("W1", 8 * RW); W2 = A.bf16("W2", 8 * RW); WA = A.bf16("WA", 8 * 768)
        stg = A.f32("stg", INW); mu_b = A.f32("mub", RW); mu_h = A.f32("muh", RW); tmpw = A.f32("tmpw", RW)
        k.dma(mu_b[:, :], mu.v(mu.ap[0:1, :].to_broadcast([128, RW])))
        k.ts("dve", mu_h[:, :], mu_b[:, :], 0.5, None, ALU.mult)
        k.ts("dve", mu_b[:, :], mu_b[:, :], -1.0, 1.0, ALU.mult, ALU.add)
        for c in range(8):
            k.dma(stg[:, :], w_in[c * 128:(c + 1) * 128, :])
            k.tt("dve", W1[:, c * RW:(c + 1) * RW], stg[:, 0:RW], mu_b[:, :], ALU.mult)
            k.tt("dve", W2[:, c * RW:(c + 1) * RW], stg[:, 0:RW], mu_h[:, :], ALU.mult)
            k.copy("act", WA[:, c * 768:(c + 1) * 768], stg[:, RW:INW])
        G = A.f32("G", 640)
        k.dma(G.v(r3(G.ap[:, 0:512], 64)), qn_d.v(qn_d.ap[0:1, :].unsqueeze(1).to_broadcast([128, 8, 64])))
        k.dma(G.v(r3(G.ap[:, 512:640], 64)), kn_d.v(kn_d.ap[0:1, :].unsqueeze(1).to_broadcast([128, 2, 64])))
        k.ts("dve", G[:, 0:512], G[:, 0:512], 0.125, None, ALU.mult)
        import os
        DBG = int(os.environ.get("DBG", "99"))
        if DBG == 0:
            k.barrier(); return
        NB = 2
        ext = [A.bf16("ext", 8 * 130) for _ in range(NB)]
        avg = [A.bf16("avg", 8 * 128) for _ in range(NB)]
        hs = [A.f32("hs", RW) for _ in range(NB)]
        qk = [A.f32("qk", 640) for _ in range(NB)]
        qsq = A.f32("qsq", 640)
        st10 = [A.f32("st10", 40) for _ in range(NB)]
        rp = [A.f32("rp", 64) for _ in range(NB)]
        t1 = A.f32("t1", 320); t2 = A.f32("t2", 320)
        qkb = [A.bf16("qkb", 640) for _ in range(NB)]
        vb = [A.bf16("vb", 128) for _ in range(NB)]
        qkT = [A.bf16("qkT", 1280) for _ in range(NB)]
        groups = [(0, 512), (512, 512), (1024, 512), (1536, 320)]
        for i in range(NT):
            b = i % NB
            e3 = r3(ext[b].ap, 130)
            lo = 1 if i == 0 else 0
            hi = 129 if i == NT - 1 else 130
            if i == 0:
                k.memset("pool", ext[b].v(e3[:, :, 0:1]), 0.0)
            if i == NT - 1:
                k.memset("pool", ext[b].v(e3[:, :, 129:130]), 0.0)
            k.dma(ext[b].v(e3[:, :, lo:hi]), hnT_d[:, :, i * 128 - 1 + lo:i * 128 - 1 + hi])
            a3 = r3(avg[b].ap, 128)
            k.tt("dve", avg[b].v(a3), ext[b].v(e3[:, :, 0:128]), ext[b].v(e3[:, :, 2:130]), ALU.add)
            for gi_, (c0, cw) in enumerate(groups):
                ps = PS[gi_ % 4]
                for c in range(8):
                    k.mm(ps[:, 0:cw], ext[b].v(e3[:, c, 1:129]), W1[:, c * RW + c0:c * RW + c0 + cw], start=(c == 0), stop=False)
                for c in range(8):
                    k.mm(ps[:, 0:cw], avg[b].v(a3[:, c, :]), W2[:, c * RW + c0:c * RW + c0 + cw], start=False, stop=(c == 7))
                k.copy("act", hs[b][:, c0:c0 + cw], ps[:, 0:cw])
            if DBG == 1:
                continue
            if DBG != 21:
                k.dma(hs_d[i * 128:(i + 1) * 128, 0:RW], hs[b][:, :])
            for gi_, (c0, cw) in enumerate(((0, 512), (512, 256))):
                ps = PS[4 + gi_]
                for c in range(8):
                    k.mm(ps[:, 0:cw], ext[b].v(e3[:, c, 1:129]), WA[:, c * 768 + c0:c * 768 + c0 + cw], start=(c == 0), stop=(c == 7))
            k.copy("act", qk[b][:, 0:512], PS[4][:, 0:512])
            k.copy("act", qk[b][:, 512:640], PS[5][:, 0:128])
            k.copy("act", vb[b][:, :], PS[5][:, 128:256])
            if DBG != 22:
                k.dma(v_d[i * 128:(i + 1) * 128, :], vb[b][:, :])
            if DBG in (2, 21, 22):
                continue
            q3 = r3(qk[b].ap, 64)
            k.act(qsq[:, :], qk[b][:, :], AF.Square)
            k.reduce(st10[b][:, 0:10], qsq.v(r3(qsq.ap, 64)))
            k.ts("dve", st10[b][:, 10:20], st10[b][:, 0:10], 1.0 / 64, EPS, ALU.mult, ALU.add)
            k.act(st10[b][:, 20:30], st10[b][:, 10:20], AF.Sqrt)
            k.recip(st10[b][:, 30:40], st10[b][:, 20:30])
            rb = st10[b].ap[:, 30:40].rearrange("p (a o) -> p a o", o=1).to_broadcast([128, 10, 64])
            k.tt("dve", qk[b].v(q3), qk[b].v(q3), st10[b].v(rb), ALU.mult)
            k.tt("dve", qk[b][:, :], qk[b][:, :], G[:, :], ALU.mult)
            if DBG == 3:
                continue
            k.dma(rp[b][:, :], rope_d[i * 128:(i + 1) * 128, :])
            q4 = qk[b].ap.rearrange("p (h i two) -> p h i two", i=32, two=2)
            x1 = qk[b].v(q4[:, :, :, 0]); x2 = qk[b].v(q4[:, :, :, 1])
            cb = rp[b].v(rp[b].ap[:, 0:32].unsqueeze(1).to_broadcast([128, 10, 32]))
            sb = rp[b].v(rp[b].ap[:, 32:64].unsqueeze(1).to_broadcast([128, 10, 32]))
            o4 = qkb[b].ap.rearrange("p (h i two) -> p h i two", i=32, two=2)
            t13 = t1.v(r3(t1.ap, 32)); t23 = t2.v(r3(t2.ap, 32))
            k.tt("dve", t13, x1, cb, ALU.mult)
            k.tt("dve", t23, x2, sb, ALU.mult)
            k.tt("dve", qkb[b].v(o4[:, :, :, 0]), t13, t23, ALU.subtract)
            k.tt("dve", t13, x1, sb, ALU.mult)
            k.tt("dve", t23, x2, cb, ALU.mult)
            k.tt("dve", qkb[b].v(o4[:, :, :, 1]), t13, t23, ALU.add)
            if DBG == 4:
                continue
            for half in range(2):
                ps = PS[6 + half]
                pb = ps.ap.bitcast(BF16)
                for hh in range(5):
                    h_ = half * 5 + hh
                    k.tr(ps.v(pb[0:64, hh * 128:(hh + 1) * 128]), qkb[b][:, h_ * 64:(h_ + 1) * 64], ident_b[:, :])
                k.copy("act" if half else "dve", qkT[b][0:64, half * 640:(half + 1) * 640], ps.v(pb[0:64, 0:640]))
            t3 = r3(qkT[b].ap, 128)
            k.dma(qT_d[:, :, i * 128:(i + 1) * 128], qkT[b].v(t3[0:64, 0:8, :]))
            k.dma(kT_d[:, :, i * 128:(i + 1) * 128], qkT[b].v(t3[0:64, 8:10, :]))
        k.barrier()

    def phase_attn():
        A.off = const_end
        kT = A.bf16("kT", T)
        vx = A.bf16("vx", NT * 2 * 65)
        for g_ in range(2):
            k.dma(kT[g_ * 64:(g_ + 1) * 64, :], kT_d[:, g_, :])
        v4 = vx.ap.rearrange("p (n g d) -> p n g d", g=2, d=65)
        k.memset("dve", vx.v(v4[:, :, :, 64:65]), 1.0)
        for g_ in range(2):
            k.dma(vx.v(v4[:, :, g_, 0:64]), v_d.v(v_d.ap[:, g_ * 64:(g_ + 1) * 64].rearrange("(n p) d -> p n d", p=128)))
        sel = A.f32("sel", 64)
        k.memset("dve", sel[:, :], 0.0)
        k.memset("dve", sel[64:65, :], 1.0)
        NQ = 3
        qt = [A.bf16("qt", 512) for _ in range(NQ)]
        NP = 3
        pT = [[A.bf16("pT", 512) for _ in range(NP)] for _ in range(2)]
        ot = [A.f32("ot", 512) for _ in range(2)]
        rinv = [A.f32("rinv", 512) for _ in range(2)]
        yb = [A.bf16("yb", 512) for _ in range(2)]
        flat = [(i_, j_) for i_ in range(NT) for j_ in range(NT)]
        nxt = [0]
        sc = [0]
        pend = []
        AHEAD = 2

        def issue_next():
            if nxt[0] >= len(flat):
                return
            i_, j_ = flat[nxt[0]]
            nxt[0] += 1
            qb = qt[i_ % NQ]
            if j_ == 0:
                for g_ in range(2):
                    k.dma(qb.v(r3(qb.ap, 128)[g_ * 64:(g_ + 1) * 64, :, :]), qT_d[:, g_ * 4:(g_ + 1) * 4, i_ * 128:(i_ + 1) * 128])
            pair = []
            for g_ in range(2):
                ps = PS[(sc[0] % 3) * 2 + g_]
                k.mm(ps[:, :], kT[g_ * 64:(g_ + 1) * 64, j_ * 128:(j_ + 1) * 128], qb[g_ * 64:(g_ + 1) * 64, :])
                pair.append(ps)
            sc[0] += 1
            pend.append(pair)

        for _ in range(AHEAD):
            issue_next()
        it = 0
        for i in range(NT):
            po = [PS[6], PS[7]]
            for j in range(NT):
                pair = pend.pop(0)
                for g_ in range(2):
                    p_ = pT[g_][it % NP]
                    k.act(p_[:, :], pair[g_][:, :], AF.Exp)
                issue_next()
                for g_ in range(2):
                    p_ = pT[g_][it % NP]
                    k.mm(po[g_][0:65, :], vx.v(v4[:, j, g_, :]), p_[:, :], start=(j == 0), stop=(j == NT - 1))
                it += 1
            for g_ in range(2):
                k.copy("dve", ot[g_][0:65, :], po[g_][0:65, :])
                pbc = po[g_]
                k.mm(pbc[0:64, :], sel[0:65, :], ot[g_][0:65, :])
                k.recip(rinv[g_][0:64, :], pbc[0:64, :])
                k.tt("dve", yb[g_][0:64, :], ot[g_][0:64, :], rinv[g_][0:64, :], ALU.mult)
                k.dma(ybT_d[:, g_ * 4:(g_ + 1) * 4, i * 128:(i + 1) * 128], yb[g_].v(r3(yb[g_].ap, 128)[0:64, :, :]))
        k.barrier()

    if stage in (1, 11, 12):
        phase_norm(x_in, 0)
        if stage != 12:
            phase_inproj()
        if stage == 1:
            phase_attn()


    s5in = {}
    for nm, shp in (("s5_lam_re_f", [64, 64]), ("s5_lam_im_f", [64, 64]), ("s5_log_dt_f", [1, 64]),
                    ("s5_lam_re_b", [64, 64]), ("s5_lam_im_b", [64, 64]), ("s5_log_dt_b", [1, 64]),
                    ("s5_b_re", [64, 64, 16]), ("s5_b_im", [64, 64, 16]),
                    ("s5_c_re_f", [64, 16, 64]), ("s5_c_im_f", [64, 16, 64]),
                    ("s5_c_re_b", [64, 16, 64]), ("s5_c_im_b", [64, 16, 64]),
                    ("s5_d", [1, D]), ("s5_glu_w", [D, D]), ("s5_glu_b", [1, D]), ("rowmask", [128, 8])):
        s5in[nm] = din(nm, shp)
    yS_d = dscr("yS", [128, 8, T])
    m_d = Buf(k, "m_s5", nc.dram_tensor("m_s5", [T, D], F32, kind="ExternalOutput" if stage == 2 else "Internal").ap())

    def phase_s5():
        NCH = 32
        for d_, sfx in enumerate(("f", "b")):
            A.off = const_end
            Ere = A.bf16("Ere", 4096); Eim = A.bf16("Eim", 4096)
            Bp = [A.bf16("Bp", 4096) for _ in range(2)]
            Cp = [A.bf16("Cp", 4096) for _ in range(2)]
            rho = A.f32("rho", NCH)
            carry = [A.f32("carry", NCH) for _ in range(2)]
            rmask = A.f32("rmask", 8)
            k.dma(rmask[:, :], s5in["rowmask"][:, :])
            if d_ == 1:
                gw = A.bf16("gw", 8 * D)
                Dd = A.bf16("Dd", 8 * 128)
                gb = A.f32("gb", 8)
                k.dma(gb[:, :], s5in["s5_glu_b"].v(s5in["s5_glu_b"].ap[0:1, :].rearrange("o (c p) -> p (o c)", p=128)), slow=True)
            work0 = A.off
            sm = {n: A.f32(n, NCH) for n in ("lre", "lim", "dt", "th", "c", "s", "c2", "s2", "t", "u", "lbr", "lbi", "nr", "den", "cre", "cim", "rd")}
            stg = A.f32("stg5", 1024)
            if d_ == 1:
                dcol = A.f32("dcol", 8)
                k.dma(dcol[:, :], s5in["s5_d"].v(s5in["s5_d"].ap[0:1, :].rearrange("o (c p) -> p (o c)", p=128)), slow=True)
                for c in range(8):
                    k.ts("dve", Dd[:, c * 128:(c + 1) * 128], ident_f[:, :], dcol[:, c:c + 1], None, ALU.mult)
                for c in range(8):
                    k.dma(stg[:, :], s5in["s5_glu_w"][c * 128:(c + 1) * 128, :])
                    k.copy("dve", gw[:, c * D:(c + 1) * D], stg[:, :])
            Bch = [A.f32("Bch", 512) for _ in range(2)]
            for ri, nm in enumerate(("s5_b_re", "s5_b_im")):
                src = s5in[nm]
                k.dma(Bch[ri].v(r3(Bch[ri].ap, 16)), src.v(src.ap.rearrange("(cc gl) p c -> (gl p) cc c", gl=2)))
            Bh = [A.f32("Bh", 512) for _ in range(2)]
            Bpf = A.f32("Bpf", 4096)
            Cn = A.f32("Cn", 512)
            Cpf = A.f32("Cpf", 4096)
            Eref = A.f32("Eref", 4096); Eimf = A.f32("Eimf", 4096)
            tA = A.f32("tA", 512); tB = A.f32("tB", 512)
            lre, lim = sm["lre"], sm["lim"]
            for dst, nm in ((lre, "s5_lam_re_" + sfx), (lim, "s5_lam_im_" + sfx)):
                src = s5in[nm]
                k.dma(dst[:, :], src.v(src.ap.rearrange("(cc gl) p -> (gl p) cc", gl=2)), slow=True)
            ld = s5in["s5_log_dt_" + sfx]
            for gl in range(2):
                k.dma(sm["dt"][gl * 64:(gl + 1) * 64, :],
                      ld.v(ld.ap[0:1, :].rearrange("o (cc gl) -> o cc gl", gl=2)[:, :, gl].to_broadcast([64, NCH])), slow=True)
            k.act(sm["dt"][:, :], sm["dt"][:, :], AF.Exp)
            k.tt("dve", sm["t"][:, :], lre[:, :], sm["dt"][:, :], ALU.mult)
            k.act(rho[:, :], sm["t"][:, :], AF.Exp)
            k.tt("dve", sm["th"][:, :], lim[:, :], sm["dt"][:, :], ALU.mult)
            k.act(sm["s"][:, :], sm["th"][:, :], AF.Sin, scale=1.0 / 32)
            k.ts("dve", sm["t"][:, :], sm["th"][:, :], 1.0 / 32, math.pi / 2, ALU.mult, ALU.add)
            k.act(sm["c"][:, :], sm["t"][:, :], AF.Sin)
            for _ in range(5):
                k.tt("dve", sm["c2"][:, :], sm["c"][:, :], sm["c"][:, :], ALU.mult)
                k.tt("dve", sm["s2"][:, :], sm["s"][:, :], sm["s"][:, :], ALU.mult)
                k.tt("dve", sm["t"][:, :], sm["c"][:, :], sm["s"][:, :], ALU.mult)
                k.tt("dve", sm["c"][:, :], sm["c2"][:, :], sm["s2"][:, :], ALU.subtract)
                k.ts("dve", sm["s"][:, :], sm["t"][:, :], 2.0, None, ALU.mult)
            k.tt("dve", sm["lbr"][:, :], rho[:, :], sm["c"][:, :], ALU.mult)
            k.tt("dve", sm["lbi"][:, :], rho[:, :], sm["s"][:, :], ALU.mult)
            k.ts("dve", sm["nr"][:, :], sm["lbr"][:, :], -1.0, None, ALU.add)
            k.tt("dve", sm["t"][:, :], lre[:, :], lre[:, :], ALU.mult)
            k.tt("dve", sm["u"][:, :], lim[:, :], lim[:, :], ALU.mult)
            k.tt("dve", sm["den"][:, :], sm["t"][:, :], sm["u"][:, :], ALU.add)
            k.recip(sm["rd"][:, :], sm["den"][:, :])
            k.tt("dve", sm["t"][:, :], sm["nr"][:, :], lre[:, :], ALU.mult)
            k.tt("dve", sm["u"][:, :], sm["lbi"][:, :], lim[:, :], ALU.mult)
            k.tt("dve", sm["t"][:, :], sm["t"][:, :], sm["u"][:, :], ALU.add)
            k.tt("dve", sm["cre"][:, :], sm["t"][:, :], sm["rd"][:, :], ALU.mult)
            k.tt("dve", sm["t"][:, :], sm["lbi"][:, :], lre[:, :], ALU.mult)
            k.tt("dve", sm["u"][:, :], sm["nr"][:, :], lim[:, :], ALU.mult)
            k.tt("dve", sm["t"][:, :], sm["t"][:, :], sm["u"][:, :], ALU.subtract)
            k.tt("dve", sm["cim"][:, :], sm["t"][:, :], sm["rd"][:, :], ALU.mult)
            e3r = r3(Eref.ap, 128); e3i = r3(Eimf.ap, 128)
            c0 = 0 if d_ == 0 else 127
            k.copy("dve", Eref.v(e3r[:, :, c0]), sm["c"][:, :])
            k.ts("dve", Eimf.v(e3i[:, :, c0]), sm["s"][:, :], -1.0, None, ALU.mult)
            n = 1
            while n < 128:
                last = n - 1 if d_ == 0 else 128 - n
                for m0 in range(0, n, 16):
                    w_ = min(16, n)
                    if d_ == 0:
                        ss_ = slice(m0, m0 + w_); dd_ = slice(n + m0, n + m0 + w_)
                    else:
                        ss_ = slice(128 - n + m0, 128 - n + m0 + w_); dd_ = slice(128 - 2 * n + m0, 128 - 2 * n + m0 + w_)
                    cr = Eref.v(e3r[:, :, last:last + 1].to_broadcast([128, NCH, w_]))
                    ci = Eimf.v(e3i[:, :, last:last + 1].to_broadcast([128, NCH, w_]))
                    ta = tA.v(r3(tA.ap, 16)[:, :, 0:w_]); tb = tB.v(r3(tB.ap, 16)[:, :, 0:w_])
                    k.tt("dve", ta, Eref.v(e3r[:, :, ss_]), cr, ALU.mult)
                    k.tt("dve", tb, Eimf.v(e3i[:, :, ss_]), ci, ALU.mult)
                    k.tt("dve", Eref.v(e3r[:, :, dd_]), ta, tb, ALU.subtract)
                    k.tt("dve", ta, Eref.v(e3r[:, :, ss_]), ci, ALU.mult)
                    k.tt("dve", tb, Eimf.v(e3i[:, :, ss_]), cr, ALU.mult)
                    k.tt("dve", Eimf.v(e3i[:, :, dd_]), ta, tb, ALU.add)
                n *= 2
            k.copy("act", Ere[:, :], Eref[:, :]); k.copy("act", Eim[:, :], Eimf[:, :])
            crb = sm["cre"].v(sm["cre"].ap.unsqueeze(2).to_broadcast([128, NCH, 16]))
            cib = sm["cim"].v(sm["cim"].ap.unsqueeze(2).to_broadcast([128, NCH, 16]))
            b3 = [Bch[ri].v(r3(Bch[ri].ap, 16)) for ri in range(2)]
            tA3 = tA.v(r3(tA.ap, 16)); tB3 = tB.v(r3(tB.ap, 16))
            k.tt("dve", tA3, b3[0], crb, ALU.mult); k.tt("dve", tB3, b3[1], cib, ALU.mult)
            k.tt("dve", Bh[0].v(r3(Bh[0].ap, 16)), tA3, tB3, ALU.subtract)
            k.tt("dve", tA3, b3[1], crb, ALU.mult); k.tt("dve", tB3, b3[0], cib, ALU.mult)
            k.tt("dve", Bh[1].v(r3(Bh[1].ap, 16)), tA3, tB3, ALU.add)
            for ri in range(2):
                k.memset("dve", Bpf[:, :], 0.0)
                for gl in range(2):
                    dst = Bpf.ap[gl * 64:(gl + 1) * 64, :].rearrange("p (a b x) -> p a b x", b=4, x=128)
                    for bb in range(4):
                        k.copy("dve", Bpf.v(dst[:, :, bb, bb * 32 + gl * 16: bb * 32 + gl * 16 + 16]),
                               Bh[ri].v(Bh[ri].ap[gl * 64:(gl + 1) * 64, :].rearrange("p (a b c) -> p a b c", b=4, c=16)[:, :, bb, :]))
                for cc in range(NCH):
                    ps = PS[cc % 2]
                    k.tr(ps[:, 0:128], Bpf[:, cc * 128:(cc + 1) * 128], ident_f[:, :])
                    k.copy("act", Bp[ri][:, cc * 128:(cc + 1) * 128], ps[:, 0:128])
            for ri, nm in enumerate(("s5_c_re_" + sfx, "s5_c_im_" + sfx)):
                src = s5in[nm]
                k.dma(Cn.v(r3(Cn.ap, 64)), src.v(src.ap.rearrange("(fc g8) co p -> (g8 co) fc p", g8=8)))
                c4 = Cpf.ap.rearrange("p (a b q) -> p a b q", b=4, q=128)
                for bb in range(4):
                    for gl in range(2):
                        k.ts("dve", Cpf.v(c4[:, :, bb, gl * 64:(gl + 1) * 64]), Cn.v(r3(Cn.ap, 64)),
                             rmask[:, bb * 2 + gl:bb * 2 + gl + 1], (1.0 if ri == 0 else -1.0), ALU.mult, ALU.mult)
                for cc in range(NCH):
                    ps = PS[2 + cc % 2]
                    k.tr(ps[:, 0:128], Cpf[:, cc * 128:(cc + 1) * 128], ident_f[:, :])
                    k.copy("act", Cp[ri][:, cc * 128:(cc + 1) * 128], ps[:, 0:128])
            for ri in range(2):
                k.memset("dve", carry[ri][:, :], 0.0)
            k.barrier()
            A.off = work0
            NB = 2
            uT = [A.bf16("uT", 1024) for _ in range(NB)]
            rhoT = A.f32("rhoT", 4096)
            r3t = r3(rhoT.ap, 128)
            k.copy("dve", rhoT.v(r3t), rho.v(rho.ap.unsqueeze(2).to_broadcast([128, NCH, 128])))
            first = 0 if d_ == 0 else 127
            last = 127 if d_ == 0 else 0
            k.memset("dve", rhoT.v(r3t[:, :, first:first + 1]), 0.0)
            W = []
            for _ in range(2):
                W.append({n: A.bf16(n, 1024) for n in ("bur", "bui", "t1", "t2", "t3", "t4", "zri", "zii", "zr", "zi", "srb", "sib")})
            tmp8 = [A.f32("tmp8", 8) for _ in range(2)]
            yf = [A.f32("yf", 1024) for _ in range(NB)]
            if d_ == 1:
                zf = A.f32("zf", 1024); zb = A.bf16("zb", 1024); q1 = A.f32("q1", 1024); q2 = A.f32("q2", 1024)
            cnt = [0]

            def mix_group(i, b, g2):
                w = W[cnt[0] % 2]
                cnt[0] += 1
                u3 = r3(uT[b].ap, 128)
                for j8 in range(8):
                    cc = g2 * 8 + j8
                    fc = cc // 4
                    k.mm(PS[0 + j8 // 4][:, (j8 % 4) * 128:(j8 % 4 + 1) * 128], Bp[0][:, cc * 128:(cc + 1) * 128], uT[b].v(u3[:, fc, :]))
                    k.mm(PS[2 + j8 // 4][:, (j8 % 4) * 128:(j8 % 4 + 1) * 128], Bp[1][:, cc * 128:(cc + 1) * 128], uT[b].v(u3[:, fc, :]))
                for hb in range(2):
                    k.copy("act", w["bur"][:, hb * 512:(hb + 1) * 512], PS[0 + hb][:, :])
                    k.copy("act", w["bui"][:, hb * 512:(hb + 1) * 512], PS[2 + hb][:, :])
                er = Ere[:, g2 * 1024:(g2 + 1) * 1024]; ei = Eim[:, g2 * 1024:(g2 + 1) * 1024]
                k.tt("dve", w["t1"][:, :], w["bur"][:, :], er, ALU.mult)
                k.tt("dve", w["t2"][:, :], w["bui"][:, :], ei, ALU.mult)
                k.tt("dve", w["t3"][:, :], w["bui"][:, :], er, ALU.mult)
                k.tt("dve", w["t4"][:, :], w["bur"][:, :], ei, ALU.mult)
                k.tt("dve", w["zri"][:, :], w["t1"][:, :], w["t2"][:, :], ALU.subtract)
                k.tt("dve", w["zii"][:, :], w["t3"][:, :], w["t4"][:, :], ALU.add)
                cs = slice(g2 * 8, (g2 + 1) * 8)
                for ci, (zin, zo) in enumerate(((w["zri"], w["zr"]), (w["zii"], w["zi"]))):
                    k.tt("dve", tmp8[ci][:, :], carry[ci][:, cs], rho[:, cs], ALU.mult)
                    zf_ = zin.v(r3(zin.ap, 128)[:, :, first])
                    k.tt("dve", zf_, zf_, tmp8[ci][:, :], ALU.add)
                    rt = rhoT[:, g2 * 1024:(g2 + 1) * 1024]
                    if d_ == 0:
                        k.scan(zo[:, :], rt, zin[:, :], 0.0)
                    else:
                        k.scan(zo[:, ::-1], rhoT.v(rhoT.ap[:, g2 * 1024:(g2 + 1) * 1024][:, ::-1]), zin[:, ::-1], 0.0)
                k.tt("dve", w["t1"][:, :], w["zr"][:, :], er, ALU.mult)
                k.tt("dve", w["t2"][:, :], w["zi"][:, :], ei, ALU.mult)
                k.tt("dve", w["t3"][:, :], w["zi"][:, :], er, ALU.mult)
                k.tt("dve", w["t4"][:, :], w["zr"][:, :], ei, ALU.mult)
                k.tt("dve", w["srb"][:, :], w["t1"][:, :], w["t2"][:, :], ALU.add)
                k.tt("dve", w["sib"][:, :], w["t3"][:, :], w["t4"][:, :], ALU.subtract)
                k.copy("act", carry[0][:, cs], w["srb"].v(r3(w["srb"].ap, 128)[:, :, last]))
                k.copy("act", carry[1][:, cs], w["sib"].v(r3(w["sib"].ap, 128)[:, :, last]))
                return w

            def mix_dir(i, b, fc, py):
                if fc % 2 == 0:
                    mix_dir.w = mix_group(i, b, fc // 2)
                w = mix_dir.w
                for j4 in range(4):
                    cc = fc * 4 + j4
                    o = (fc % 2) * 512 + j4 * 128
                    k.mm(py[:, 0:128], Cp[0][:, cc * 128:(cc + 1) * 128], w["srb"][:, o:o + 128], start=(j4 == 0), stop=False)
                    k.mm(py[:, 0:128], Cp[1][:, cc * 128:(cc + 1) * 128], w["sib"][:, o:o + 128], start=False, stop=(j4 == 3 and d_ == 0))

            if d_ == 0:
                for i in range(NT):
                    b = i % NB
                    k.dma(uT[b].v(r3(uT[b].ap, 128)), hnT_d[:, :, i * 128:(i + 1) * 128])
                    for fc in range(8):
                        py = PS[4 + fc % 2]
                        mix_dir(i, b, fc, py)
                        k.copy("act", yf[b][:, fc * 128:(fc + 1) * 128], py[:, 0:128])
                    k.dma(yS_d[:, :, i * 128:(i + 1) * 128], yf[b].v(r3(yf[b].ap, 128)))
            else:
                for i in range(NT - 1, -1, -1):
                    b = i % NB
                    u3 = r3(uT[b].ap, 128)
                    k.dma(uT[b].v(u3), hnT_d[:, :, i * 128:(i + 1) * 128])
                    k.dma(yf[b].v(r3(yf[b].ap, 128)), yS_d[:, :, i * 128:(i + 1) * 128])
                    for fc in range(8):
                        py = PS[4 + fc % 2]
                        mix_dir(i, b, fc, py)
                        k.mm(py[:, 0:128], Dd[:, fc * 128:(fc + 1) * 128], uT[b].v(u3[:, fc, :]), start=False, stop=True)
                        k.tt("dve", zf[:, fc * 128:(fc + 1) * 128], py[:, 0:128], yf[b][:, fc * 128:(fc + 1) * 128], ALU.add)
                    k.act(q1[:, :], zf[:, :], AF.Square)
                    k.ts("dve", q1[:, :], q1[:, :], 0.044715, 1.0, ALU.mult, ALU.add)
                    k.tt("dve", q2[:, :], q1[:, :], zf[:, :], ALU.mult)
                    k.act(q1[:, :], q2[:, :], AF.Sigmoid, scale=1.5957691216057308)
                    k.tt("dve", zf[:, :], zf[:, :], q1[:, :], ALU.mult)
                    k.copy("act", zb[:, :], zf[:, :])
                    for fo in range(8):
                        pg = PS[6 + fo % 2]
                        for c in range(8):
                            k.mm(pg[:, 0:128], gw[:, c * D + fo * 128:c * D + (fo + 1) * 128], zb[:, c * 128:(c + 1) * 128], start=(c == 0), stop=(c == 7))
                        k.act(q2[:, fo * 128:(fo + 1) * 128], pg[:, 0:128], AF.Sigmoid, bias=gb[:, fo:fo + 1])
                    k.tt("dve", q2[:, :], zf[:, :], q2[:, :], ALU.mult)
                    pt = PS[6]
                    for half in range(2):
                        for c in range(4):
                            cc = half * 4 + c
                            k.tr(pt[:, c * 128:(c + 1) * 128], q2[:, cc * 128:(cc + 1) * 128], ident_f[:, :])
                        k.copy("act", q1[:, half * 512:(half + 1) * 512], pt[:, :])
                    k.dma(m_d[i * 128:(i + 1) * 128, :], q1[:, :])
            k.barrier()

    if stage == 2:
        phase_norm(x_in, 1)
        phase_s5()


    rin = {}
    for nm, shp in (("rwkv_w0_f", [1, 512]), ("rwkv_w_up_f", [64, 512]), ("rwkv_w0_b", [1, 512]), ("rwkv_w_up_b", [64, 512]),
                    ("rwkv_a0", [1, 512]), ("rwkv_a_up", [64, 512]), ("rwkv_g_up", [128, 512]), ("rwkv_k_k", [1, 512]),
                    ("rwkv_k_a", [1, 512]), ("rwkv_r_k", [1, 512]), ("rwkv_lnx_g", [1, 512]), ("rwkv_lnx_b", [1, 512]),
                    ("trimask", [128, 512]), ("chunkmask", [128, 2])):
        rin[nm] = din(nm, shp)
    yF_d = dscr("yF", [T, 512])
    yaT_d = Buf(k, "yaT", nc.dram_tensor("yaT", [128, 4, T], BF16, kind="ExternalOutput" if stage == 3 else "Internal").ap())
    LOGW_SCALE = -math.exp(-0.5)

    def phase_rwkv():
        A.off = const_end
        tri = A.f32("tri", 512)
        k.dma(tri[:, :], rin["trimask"][:, :])
        cmask = A.f32("cmask", 2)
        k.dma(cmask[:, :], rin["chunkmask"][:, :])
        LE, LT, GE, GT = (tri[:, j * 128:(j + 1) * 128] for j in range(4))
        gmask = [A.f32("gmask", 512) for _ in range(2)]
        nmask = [A.f32("nmask", 512) for _ in range(2)]
        for d_, (m1, m2, m3) in enumerate(((1, 0, 3), (3, 2, 1))):
            for j, mi in enumerate((m1, m2, m3, m3)):
                k.copy("dve", gmask[d_][:, j * 128:(j + 1) * 128], tri[:, mi * 128:(mi + 1) * 128])
            for j in range(4):
                k.copy("dve", nmask[d_][:, j * 128:(j + 1) * 128], tri[:, m2 * 128:(m2 + 1) * 128])
        identb4 = A.bf16("identb4", 512)
        for j in range(4):
            k.copy("dve", identb4[:, j * 128:(j + 1) * 128], ident_f[:, :])
        bc = {}
        for nm in ("rwkv_w0_f", "rwkv_w0_b", "rwkv_a0", "rwkv_k_k", "rwkv_k_a", "rwkv_r_k", "rwkv_lnx_g", "rwkv_lnx_b"):
            bc[nm] = A.f32(nm, 512)
            k.dma(bc[nm][:, :], rin[nm].v(rin[nm].ap[0:1, :].to_broadcast([128, 512])))
        omka = A.f32("omka", 512)
        k.ts("dve", omka[:, :], bc["rwkv_k_a"][:, :], -1.0, 1.0, ALU.mult, ALU.add)
        stg = A.f32("stgr", 512)
        wup = A.bf16("wup", 512); aup = A.bf16("aup", 512); gup = A.bf16("gup", 512)
        k.dma(stg[0:64, :], rin["rwkv_w_up_f"][:, :]); k.dma(stg[64:128, :], rin["rwkv_w_up_b"][:, :])
        k.copy("dve", wup[:, :], stg[:, :])
        k.dma(stg[0:64, :], rin["rwkv_a_up"][:, :])
        k.copy("dve", aup[0:64, :], stg[0:64, :])
        k.dma(stg[:, :], rin["rwkv_g_up"][:, :])
        k.copy("dve", gup[:, :], stg[:, :])
        hsb = [A.f32("hsb", RW) for _ in range(2)]
        Lb = A.bf16("Lb", 384); LTb = A.bf16("LTb", 384)
        sig = A.f32("sig", 512); asg = A.f32("asg", 512); gg = A.f32("gg", 512)
        kk = A.f32("kk", 512); sq = A.f32("sq", 512); st8 = A.f32("st8", 32)
        kmod = A.f32("kmod", 512); bb = A.f32("bb", 512); tmp = A.f32("tmp", 512); bonus = A.f32("bonus", 512)
        logw = A.f32("logw", 512)
        eP = A.f32("eP", 512); eM = A.f32("eM", 512); eX = A.f32("eX", 512); eR = A.f32("eR", 512)
        WC = A.f32("WC", 16)
        tm = {n: A.bf16(n, 512) for n in ("rt", "kt", "bt", "at", "kp", "bp", "v", "vc0", "vc1")}
        fm = {n: A.bf16("fm" + n, 1024) for n in ("at", "rt", "bt", "kt")}
        G = [A.bf16("G", 512) for _ in range(8)]
        Nkr = A.bf16("Nkr", 1024)
        X = [A.bf16("X", 1024) for _ in range(2)]; XT = [A.bf16("XT", 1024) for _ in range(2)]
        Am = [A.bf16("Am", 1024) for _ in range(2)]
        Atp = A.bf16("Atp", 1024)
        Pm = A.bf16("Pm", 1024)
        Uc = A.bf16("Uc", 512)
        Sf = A.f32("Sf", 512); Sb = A.bf16("Sb", 512); Stmp = A.f32("Stmp", 512)
        yacc = A.f32("yacc", 512)
        yfl = A.f32("yfl", 512)
        yab = A.bf16("yab", 512); yaT = A.bf16("yaT", 512)
        assert fm["rt"].ap.offset == fm["at"].ap.offset + 1024 or True

        def prep(i, d_):
            b = i % 2
            h_ = hsb[b]
            k.dma(h_[:, :], hs_d[i * 128:(i + 1) * 128, 0:RW])
            r_ = h_[:, 0:512]; k_ = h_[:, 512:1024]; v_ = h_[:, 1024:1536]
            k.act(Lb[:, 0:128], h_[:, 1536:1664], AF.Tanh)
            k.copy("dve", Lb[:, 128:192], h_[:, 1664:1728])
            k.act(Lb[:, 256:384], h_[:, 1728:1856], AF.Sigmoid)
            pt = PS[0]; ptb = pt.ap.bitcast(BF16)
            k.tr(pt.v(ptb[:, 0:128]), Lb[:, 0:128], ident_b[:, :])
            k.tr(pt.v(ptb[0:64, 128:256]), Lb[:, 128:192], ident_b[:, :])
            k.tr(pt.v(ptb[:, 256:384]), Lb[:, 256:384], ident_b[:, :])
            k.copy("dve", LTb[:, 0:128], pt.v(ptb[:, 0:128]))
            k.copy("dve", LTb[0:64, 128:256], pt.v(ptb[0:64, 128:256]))
            k.copy("dve", LTb[:, 256:384], pt.v(ptb[:, 256:384]))
            hp = d_ * 64
            k.mm(PS[1][:, :], LTb[hp:hp + 64, 0:128], wup[hp:hp + 64, :])
            k.mm(PS[2][:, :], LTb[0:64, 128:256], aup[0:64, :])
            k.mm(PS[3][:, :], LTb[:, 256:384], gup[:, :])
            k.tt("dve", tmp[:, :], PS[1][:, :], bc["rwkv_w0_b" if d_ else "rwkv_w0_f"][:, :], ALU.add)
            k.act(sig[:, :], tmp[:, :], AF.Sigmoid)
            k.ts("dve", logw[:, :], sig[:, :], LOGW_SCALE, None, ALU.mult)
            k.tt("dve", tmp[:, :], PS[2][:, :], bc["rwkv_a0"][:, :], ALU.add)
            k.act(asg[:, :], tmp[:, :], AF.Sigmoid)
            k.copy("dve", gg[:, :], PS[3][:, :])
            k.tt("dve", kk[:, :], k_, bc["rwkv_k_k"][:, :], ALU.mult)
            k.act(sq[:, :], kk[:, :], AF.Square)
            k.reduce(st8[:, 0:8], sq.v(r3(sq.ap, 64)))
            k.ts("dve", st8[:, 8:16], st8[:, 0:8], 1e-24, None, ALU.add)
            k.act(st8[:, 16:24], st8[:, 8:16], AF.Sqrt)
            k.recip(st8[:, 24:32], st8[:, 16:24])
            k.tt("dve", kk.v(r3(kk.ap, 64)), kk.v(r3(kk.ap, 64)),
                 st8.v(st8.ap[:, 24:32].unsqueeze(2).to_broadcast([128, 8, 64])), ALU.mult)
            k.tt("dve", tmp[:, :], asg[:, :], bc["rwkv_k_a"][:, :], ALU.mult)
            k.tt("dve", tmp[:, :], tmp[:, :], omka[:, :], ALU.add)
            k.tt("dve", kmod[:, :], k_, tmp[:, :], ALU.mult)
            k.tt("dve", bb[:, :], kk[:, :], asg[:, :], ALU.mult)
            mi, me, mr = ((LE, LT, GT) if d_ == 0 else (GE, GT, LT))
            k.mm(PS[4][:, :], mi, logw[:, :])
            k.mm(PS[5][:, :], me, logw[:, :])
            k.mm(PS[6][:, :], mr, logw[:, :])
            k.act(eP[:, :], PS[4][:, :], AF.Exp)
            k.act(eM[:, :], PS[4][:, :], AF.Exp, scale=-1.0)
            k.act(eX[:, :], PS[5][:, :], AF.Exp)
            k.act(eR[:, :], PS[6][:, :], AF.Exp)
            pw = PS[7]
            for hd in range(8):
                k.mm(pw[0:64, hd * 2:(hd + 1) * 2], logw[:, hd * 64:(hd + 1) * 64], cmask[:, :])
            k.act(WC[0:64, :], pw[0:64, 0:16], AF.Exp)
            k.tt("dve", tm["rt"][:, :], r_, eP[:, :], ALU.mult)
            k.tt("dve", tm["kt"][:, :], kmod[:, :], eM[:, :], ALU.mult)
            k.tt("dve", tm["bt"][:, :], bb[:, :], eM[:, :], ALU.mult)
            k.stt("dve", tm["at"][:, :], kk[:, :], -1.0, eX[:, :], ALU.mult, ALU.mult)
            k.tt("dve", tm["kp"][:, :], kmod[:, :], eR[:, :], ALU.mult)
            k.tt("dve", tm["bp"][:, :], bb[:, :], eR[:, :], ALU.mult)
            k.copy("act", tm["v"][:, :], v_)
            k.act(tm["vc0"][:, :], v_, AF.Copy, scale=cmask[:, 0:1])
            k.act(tm["vc1"][:, :], v_, AF.Copy, scale=cmask[:, 1:2])
            if d_ == 1:
                k.tt("dve", tmp[:, :], r_, kmod[:, :], ALU.mult)
                k.tt("dve", tmp[:, :], tmp[:, :], bc["rwkv_r_k"][:, :], ALU.mult)
                k.reduce(st8[:, 0:8], tmp.v(r3(tmp.ap, 64)))
                k.tt("dve", bonus.v(r3(bonus.ap, 64)), h_.v(r3(h_.ap[:, 1024:1536], 64)),
                     st8.v(st8.ap[:, 0:8].unsqueeze(2).to_broadcast([128, 8, 64])), ALU.mult)
            for ti, n in enumerate(("at", "rt", "bt", "kt")):
                pt_ = PS[ti % 2]; pb_ = pt_.ap.bitcast(BF16)
                for hd in range(8):
                    k.tr(pt_.v(pb_[0:64, hd * 128:(hd + 1) * 128]), tm[n][:, hd * 64:(hd + 1) * 64], ident_b[:, :])
                k.copy("dve" if ti % 2 else "act", fm[n][0:64, :], pt_.v(pb_[0:64, 0:1024]))

        def chunk_alg(i, d_):
            for hd in range(8):
                hs_ = slice(hd * 128, (hd + 1) * 128)
                pg = PS[2 + hd % 2]
                k.mm(pg[:, 0:128], fm["bt"][0:64, hs_], fm["at"][0:64, hs_])
                k.mm(pg[:, 128:256], fm["bt"][0:64, hs_], fm["rt"][0:64, hs_])
                k.mm(pg[:, 256:384], fm["at"][0:64, hs_], fm["bt"][0:64, hs_])
                k.mm(pg[:, 384:512], fm["at"][0:64, hs_], fm["kt"][0:64, hs_])
                k.tt("dve", G[hd][:, :], pg[:, :], gmask[d_][:, :], ALU.mult)
                pn = PS[4]
                k.mm(pn[:, (hd % 4) * 128:(hd % 4 + 1) * 128], fm["kt"][0:64, hs_], fm["rt"][0:64, hs_])
                if hd % 4 == 3:
                    k.tt("dve", Nkr[:, (hd // 4) * 512:(hd // 4 + 1) * 512], pn[:, :], nmask[d_][:, :], ALU.mult)
            for g4 in range(2):
                gs = slice(g4 * 512, (g4 + 1) * 512)
                for hh in range(4):
                    hd = g4 * 4 + hh
                    k.copy("act", X[0][:, hd * 128:(hd + 1) * 128], G[hd][:, 0:128])
                    k.copy("act", XT[0][:, hd * 128:(hd + 1) * 128], G[hd][:, 256:384])
                k.tt("dve", Am[0][:, gs], X[0][:, gs], identb4[:, :], ALU.add)
            cur = 0
            for lev in range(5):
                nxt = 1 - cur
                for g4 in range(2):
                    gs = slice(g4 * 512, (g4 + 1) * 512)
                    for hh in range(4):
                        hd = g4 * 4 + hh
                        hs_ = slice(hd * 128, (hd + 1) * 128); ps_ = slice(hh * 128, (hh + 1) * 128)
                        if lev < 4:
                            k.mm(PS[5][:, ps_], XT[cur][:, hs_], X[cur][:, hs_])
                        k.mm(PS[6][:, ps_], X[cur][:, hs_], XT[cur][:, hs_])
                    if lev < 4:
                        k.copy("act", X[nxt][:, gs], PS[5][:, :])
                    k.copy("dve", XT[nxt][:, gs], PS[6][:, :])
                    for hh in range(4):
                        hd = g4 * 4 + hh
                        hs_ = slice(hd * 128, (hd + 1) * 128); ps_ = slice(hh * 128, (hh + 1) * 128)
                        k.mm(PS[7][:, ps_], XT[nxt][:, hs_], Am[cur][:, hs_])
                    k.tt("dve", Am[nxt][:, gs], PS[7][:, :], Am[cur][:, gs], ALU.add)
                cur = nxt
            Tinv = Am[cur]
            for g4 in range(2):
                for hh in range(4):
                    hd = g4 * 4 + hh
                    hs_ = slice(hd * 128, (hd + 1) * 128); ps_ = slice(hh * 128, (hh + 1) * 128)
                    k.mm(PS[0][0:64, ps_], tm["at"][:, hd * 64:(hd + 1) * 64], Tinv[:, hs_])
                    k.mm(PS[1][:, ps_], G[hd][:, 384:512], Tinv[:, hs_])
                k.copy("act", Atp[0:64, g4 * 512:(g4 + 1) * 512], PS[0][0:64, :])
                k.copy("dve", Pm[:, g4 * 512:(g4 + 1) * 512], PS[1][:, :])
            k.memset("dve", yacc[:, :], 0.0)
            for c in ((0, 1) if d_ == 0 else (1, 0)):
                vc = tm["vc%d" % c]
                pu, py, pd = PS[5], PS[6], PS[7]
                for hd in range(8):
                    hs_ = slice(hd * 128, (hd + 1) * 128); vs_ = slice(hd * 64, (hd + 1) * 64)
                    k.mm(pu[:, vs_], Atp[0:64, hs_], Sb[0:64, vs_], start=True, stop=False)
                    k.mm(pu[:, vs_], Pm[:, hs_], tm["v"][:, vs_], start=False, stop=True)
                k.ts("dve", Uc[:, :], pu[:, :], cmask[:, c:c + 1], None, ALU.mult)
                for hd in range(8):
                    hs_ = slice(hd * 128, (hd + 1) * 128); vs_ = slice(hd * 64, (hd + 1) * 64)
                    k.mm(py[:, vs_], fm["rt"][0:64, hs_], Sb[0:64, vs_], start=True, stop=False)
                    k.mm(py[:, vs_], G[hd][:, 128:256], Uc[:, vs_], start=False, stop=False)
                    k.mm(py[:, vs_], Nkr[:, hs_], tm["v"][:, vs_], start=False, stop=True)
                    k.mm(pd[0:64, vs_], tm["bp"][:, vs_], Uc[:, vs_], start=True, stop=False)
                    k.mm(pd[0:64, vs_], tm["kp"][:, vs_], vc[:, vs_], start=False, stop=True)
                k.stt("dve", yacc[:, :], py[:, :], cmask[:, c:c + 1], yacc[:, :], ALU.mult, ALU.add)
                wcb = WC.v(WC.ap[0:64, :].rearrange("p (h c) -> p h c", c=2)[:, :, c:c + 1].to_broadcast([64, 8, 64]))
                k.tt("dve", Stmp.v(r3(Stmp.ap[0:64, :], 64)), Sf.v(r3(Sf.ap[0:64, :], 64)), wcb, ALU.mult)
                k.tt("dve", Sf[0:64, :], Stmp[0:64, :], pd[0:64, :], ALU.add)
                k.copy("act", Sb[0:64, :], Sf[0:64, :])

        for d_ in range(2):
            k.memset("dve", Sf[:, :], 0.0)
            k.memset("dve", Sb[:, :], 0.0)
            order = range(NT) if d_ == 0 else range(NT - 1, -1, -1)
            for i in order:
                prep(i, d_)
                chunk_alg(i, d_)
                if d_ == 0:
                    k.dma(yF_d[i * 128:(i + 1) * 128, :], yacc[:, :])
                else:
                    k.dma(yfl[:, :], yF_d[i * 128:(i + 1) * 128, :])
                    k.tt("dve", yfl[:, :], yfl[:, :], yacc[:, :], ALU.add)
                    y3 = yfl.v(r3(yfl.ap, 64))
                    k.reduce(st8[:, 0:8], y3)
                    k.ts("dve", st8[:, 8:16], st8[:, 0:8], 1.0 / 64, None, ALU.mult)
                    k.tt("dve", y3, y3, st8.v(st8.ap[:, 8:16].unsqueeze(2).to_broadcast([128, 8, 64])), ALU.subtract)
                    k.act(sq[:, :], yfl[:, :], AF.Square)
                    k.reduce(st8[:, 0:8], sq.v(r3(sq.ap, 64)))
                    k.ts("dve", st8[:, 8:16], st8[:, 0:8], 1.0 / 64, 64e-5, ALU.mult, ALU.add)
                    k.act(st8[:, 16:24], st8[:, 8:16], AF.Sqrt)
                    k.recip(st8[:, 24:32], st8[:, 16:24])
                    k.tt("dve", y3, y3, st8.v(st8.ap[:, 24:32].unsqueeze(2).to_broadcast([128, 8, 64])), ALU.mult)
                    k.tt("dve", yfl[:, :], yfl[:, :], bc["rwkv_lnx_g"][:, :], ALU.mult)
                    k.tt("dve", yfl[:, :], yfl[:, :], bc["rwkv_lnx_b"][:, :], ALU.add)
                    k.tt("dve", yfl[:, :], yfl[:, :], bonus[:, :], ALU.add)
                    k.tt("dve", yab[:, :], yfl[:, :], gg[:, :], ALU.mult)
                    pt_ = PS[0]; pb_ = pt_.ap.bitcast(BF16)
                    for c in range(4):
                        k.tr(pt_.v(pb_[:, c * 128:(c + 1) * 128]), yab[:, c * 128:(c + 1) * 128], ident_b[:, :])
                    k.copy("act", yaT[:, :], pt_.v(pb_[:, 0:512]))
                    k.dma(yaT_d[:, :, i * 128:(i + 1) * 128], yaT.v(r3(yaT.ap, 128)))
        k.barrier()

    if stage == 3:
        phase_norm(x_in, 0)
        phase_inproj()
        phase_rwkv()

    def outproj_alloc():
        wo = A.bf16("wo_b", 8 * D)
        woa = A.bf16("wo_a", 4 * D)
        stg = A.f32("stgo", D // 2)
        for h_ in range(16):
            hh, hf = h_ // 2, h_ % 2
            k.dma(stg[0:64, :], w_out[512 + hh * 64:512 + (hh + 1) * 64, hf * 512:(hf + 1) * 512])
            k.copy("dve", wo[0:64, hh * D + hf * 512:hh * D + (hf + 1) * 512], stg[0:64, :])
        for h_ in range(8):
            c, hf = h_ // 2, h_ % 2
            k.dma(stg[:, :], w_out[c * 128:(c + 1) * 128, hf * 512:(hf + 1) * 512])
            k.copy("dve", woa[:, c * D + hf * 512:c * D + (hf + 1) * 512], stg[:, :])
        ybt = A.bf16("ybt", 1024)
        yat = A.bf16("yat", 512)
        return wo, woa, ybt, yat

    def outproj_pre(state, i, xtb):
        wo, woa, yb, ya = state
        y3 = r3(yb.ap, 128)
        a3 = r3(ya.ap, 128)
        k.dma(yb.v(y3[0:64, :, :]), ybT_d[:, :, i * 128:(i + 1) * 128])
        k.dma(ya.v(a3), yaT_d[:, :, i * 128:(i + 1) * 128])
        for j in range(2):
            ps = PS[6 + j]
            for c in range(4):
                k.mm(ps[:, :], ya.v(a3[:, c, :]), woa[:, c * D + j * 512:c * D + (j + 1) * 512], start=(c == 0), stop=False)
            for h_ in range(8):
                k.mm(ps[:, :], yb.v(y3[0:64, h_, :]), wo[0:64, h_ * D + j * 512:h_ * D + (j + 1) * 512], start=False, stop=(h_ == 7))
            k.tt("dve", xtb[:, j * 512:(j + 1) * 512], ps[:, :], xtb[:, j * 512:(j + 1) * 512], ALU.add)

    def s5add_alloc():
        return A.f32("mbuf", D)

    def s5add_pre(mb, i, xtb):
        k.dma(mb[:, :], m_d[i * 128:(i + 1) * 128, :])
        k.tt("dve", xtb[:, :], xtb[:, :], mb[:, :], ALU.add)

    if stage == 99:
        phase_norm(x_in, 0)
        phase_inproj()
        phase_attn()
        phase_rwkv()
        ffn_phase(0, x_in, x1_d, outproj_alloc, outproj_pre)
        phase_norm(x1_d, 1)
        phase_s5()
        ffn_phase(1, x1_d, y_out, s5add_alloc, s5add_pre)

    if stage == 0:
        ffn_phase(0, x_in, y_out)

    k.barrier()
    with nc.Block() as block:
        k.replay(block)
    return nc, st


def _rope_tab(T):
    t = np.arange(T)
    inv = (10000.0 ** (-np.arange(16, dtype=np.float32) / 16)).astype(np.float32)
    ang = np.concatenate([(t // 64)[:, None].astype(np.float32) * inv,
                          (t % 64)[:, None].astype(np.float32) * inv], -1)
    return np.concatenate([np.cos(ang), np.sin(ang)], -1).astype(np.float32)


def common_inputs(inputs, T):
    f = lambda n: np.ascontiguousarray(np.asarray(inputs[n], dtype=np.float32))
    common = dict(mix_norm=f("mix_norm"), ffn_norm=f("ffn_norm"), ffn_up=f("ffn_up"), ffn_down=f("ffn_down"),
                  hyb_w_in=f("hyb_w_in")[0], hyb_shift_mu=f("hyb_shift_mu"), hyb_w_out=f("hyb_w_out")[0],
                  identb=np.eye(128, dtype=np.float32), rope=_rope_tab(T),
                  att_q_norm=f("att_q_norm"), att_k_norm=f("att_k_norm"),
                  rowmask=(np.arange(128)[:, None] // 16 == np.arange(8)[None, :]).astype(np.float32))
    for n in ("s5_lam_re_f", "s5_lam_im_f", "s5_log_dt_f", "s5_lam_re_b", "s5_lam_im_b", "s5_log_dt_b",
              "s5_b_re", "s5_b_im", "s5_c_re_f", "s5_c_im_f", "s5_c_re_b", "s5_c_im_b", "s5_glu_w"):
        a = f(n)
        common[n] = a[0] if n != "s5_log_dt_f" and n != "s5_log_dt_b" else a
    for n in ("s5_d", "s5_glu_b"):
        common[n] = f(n)
    for n in ("rwkv_w0_f", "rwkv_w0_b", "rwkv_a0", "rwkv_k_k", "rwkv_k_a", "rwkv_lnx_g", "rwkv_lnx_b"):
        common[n] = f(n)
    common["rwkv_r_k"] = f("rwkv_r_k").reshape(1, 512)
    for n in ("rwkv_w_up_f", "rwkv_w_up_b", "rwkv_a_up", "rwkv_g_up"):
        common[n] = f(n)[0]
    t = np.arange(128)
    same = (t[:, None] // 64) == (t[None, :] // 64)
    tri = [same & (t[:, None] <= t[None, :]), same & (t[:, None] < t[None, :]),
           same & (t[:, None] >= t[None, :]), same & (t[:, None] > t[None, :])]
    common["trimask"] = np.concatenate(tri, 1).astype(np.float32)
    common["chunkmask"] = (t[:, None] // 64 == np.arange(2)[None, :]).astype(np.float32)
    return common


def kernel(**inputs):
    xp = np.asarray(inputs["x_prompt"]); xs = np.asarray(inputs["x_sample"])
    T = xp.shape[1]
    seqs = [xp[i] for i in range(xp.shape[0])] + [xs[i] for i in range(xs.shape[0])]
    nseq = len(seqs)
    nc, st = build(T, stage=99)
    common = common_inputs(inputs, T)
    in_maps = []
    for c in range(NCORES):
        m = dict(common)
        m["x"] = np.ascontiguousarray(seqs[c % nseq], dtype=np.float32)
        in_maps.append(m)
    res = run_bass_kernel_spmd(nc, in_maps, core_ids=list(range(NCORES)))
    ys = [np.asarray(res.results[c]["y"], dtype=np.float32) for c in range(nseq)]
    yp = np.stack(ys[:xp.shape[0]], 0)
    ysm = np.stack(ys[xp.shape[0]:], 0)
    return (yp, ysm)
```
